# Optimizing a Trainium2 kernel written in Bass

```python
import jax, jax.numpy as jnp
from jax import lax
import numpy as np

D_MODEL = 2048
BATCH = 4
SEQ = 2048
DEPTH = 2
DEC_BATCH = 128
DEC_SEQ = 4
PAST_LEN = 16384
PAGE_SIZE = 128

D_LRU = D_MODEL // 2
D_SC = D_MODEL - D_LRU
D_MIX = D_LRU + D_SC
LRU_HEADS = 8
LRU_HEAD_DIM = D_LRU // LRU_HEADS
SC_GROUPS = 8
LRU_CONV_W = 4
SC_CONV_W = 3
RG_LRU_C = 8.0
D_PLE = 256
IN_SPLITS = (D_LRU, D_LRU, D_SC, D_SC, D_SC, D_SC)
IN_WIDTH = sum(IN_SPLITS)
DEEPNORM_ALPHA = (2.0 * DEPTH) ** 0.25
DEEPNORM_BETA = (8.0 * DEPTH) ** -0.25
LN_EPS = 1e-5
GN_EPS = 1e-6

kernel_name = 'hybrid_rglru_shortconv_deepnorm_step'


def causal_dwconv(x, buf, w):
    k_width = w.shape[0]
    t_len = x.shape[1]
    xp = jnp.concatenate([buf.astype(x.dtype), x], axis=1)
    y = sum(xp[:, k:k + t_len] * w[k] for k in range(k_width))
    return y, xp[:, -(k_width - 1):]


def group_rmsnorm(y, g, n_groups):
    b, t, c = y.shape
    yf = y.astype(jnp.float32).reshape(b, t, n_groups, c // n_groups)
    yf = yf * lax.rsqrt(jnp.mean(yf * yf, axis=-1, keepdims=True) + GN_EPS)
    return (yf.reshape(b, t, c) * g.astype(jnp.float32)).astype(y.dtype)


def layer_norm(x, g, b):
    xf = x.astype(jnp.float32)
    mu = jnp.mean(xf, axis=-1, keepdims=True)
    var = jnp.mean(jnp.square(xf - mu), axis=-1, keepdims=True)
    y = (xf - mu) * lax.rsqrt(var + LN_EPS)
    return (y * g.astype(jnp.float32) + b.astype(jnp.float32)).astype(x.dtype)


def rg_lru(xc, w_a, b_a, w_x, b_x, lam, h0):
    b, t, c = xc.shape
    xh = xc.reshape(b, t, LRU_HEADS, LRU_HEAD_DIM)
    r = jax.nn.sigmoid(jnp.einsum('bthi,hij->bthj', xh, w_a) + b_a).reshape(b, t, c)
    i = jax.nn.sigmoid(jnp.einsum('bthi,hij->bthj', xh, w_x) + b_x).reshape(b, t, c)
    log_a = -RG_LRU_C * r.astype(jnp.float32) * jax.nn.softplus(-lam.astype(jnp.float32))
    a = jnp.exp(log_a)
    mult = jnp.sqrt(jnp.maximum(-jnp.expm1(2.0 * log_a), 0.0))
    u = mult * (i * xc).astype(jnp.float32)

    def step(h, au):
        a_t, u_t = au
        h = a_t * h + u_t
        return h, h

    h_last, hs = lax.scan(step, h0.astype(jnp.float32),
                          (jnp.swapaxes(a, 0, 1), jnp.swapaxes(u, 0, 1)))
    return jnp.swapaxes(hs, 0, 1).astype(xc.dtype), h_last.astype(h0.dtype)


def hybrid_layer(x, p, h0, lbuf, sbuf, w_in, lru_conv_w, lru_conv_b, lru_wa, lru_ba,
                 lru_wx, lru_bx, lru_lambda, sc_conv_w, gn_lru, gn_sc, w_out,
                 ple_wp, ple_wg, ple_bg, ln_g, ln_b):
    z = jnp.einsum('btd,de->bte', x, w_in)
    x_l, g_l, b_s, c_s, h_s, g_s = jnp.split(z, np.cumsum(IN_SPLITS)[:-1].tolist(), axis=-1)
    xc, lbuf_new = causal_dwconv(x_l, lbuf, lru_conv_w)
    xc = xc + lru_conv_b
    hs, h_last = rg_lru(xc, lru_wa, lru_ba, lru_wx, lru_bx, lru_lambda, h0)
    y_l = hs * jax.nn.silu(g_l)
    v, sbuf_new = causal_dwconv(c_s * h_s, sbuf, sc_conv_w)
    y_s = b_s * v * jax.nn.silu(g_s)
    y = jnp.concatenate([group_rmsnorm(y_l, gn_lru, LRU_HEADS),
                         group_rmsnorm(y_s, gn_sc, SC_GROUPS)], axis=-1)
    m = jnp.einsum('bte,ed->btd', y, w_out)
    r = DEEPNORM_ALPHA * x + m
    e = jnp.einsum('btk,kd->btd', p, ple_wp)
    gate = jax.nn.sigmoid(jnp.einsum('btd,de->bte', r, ple_wg) + ple_bg)
    x_new = layer_norm(r + gate * e, ln_g, ln_b)
    return x_new, h_last, lbuf_new, sbuf_new


def run_trunk(x, p, h0s, lbufs, sbufs, w_in, lru_conv_w, lru_conv_b, lru_wa, lru_ba,
              lru_wx, lru_bx, lru_lambda, sc_conv_w, gn_lru, gn_sc, w_out,
              ple_wp, ple_wg, ple_bg, ln_g, ln_b):
    hs, lbs, sbs = [], [], []
    for l in range(DEPTH):
        x, h_new, lb_new, sb_new = hybrid_layer(
            x, p[l], h0s[l], lbufs[l], sbufs[l], w_in[l], lru_conv_w[l], lru_conv_b[l],
            lru_wa[l], lru_ba[l], lru_wx[l], lru_bx[l], lru_lambda[l], sc_conv_w[l],
            gn_lru[l], gn_sc[l], w_out[l], ple_wp[l], ple_wg[l], ple_bg[l], ln_g[l], ln_b[l])
        hs.append(h_new)
        lbs.append(lb_new)
        sbs.append(sb_new)
    return x, jnp.stack(hs), jnp.stack(lbs), jnp.stack(sbs)


def setup_inputs(seed: int = 0) -> dict:
    key = jax.random.key(seed)
    ks = jax.random.split(key, 24)
    f32 = jnp.float32

    def nrm(k, shape, scale):
        return jax.random.normal(k, shape, f32) * scale

    a0 = jax.random.uniform(ks[14], (DEPTH, D_LRU), f32, 0.9, 0.999)
    a_base = a0 ** (1.0 / RG_LRU_C)
    return {
        'x_prompt': nrm(ks[0], (BATCH, SEQ, D_MODEL), 1.0),
        'x_sample': nrm(ks[1], (DEC_BATCH, DEC_SEQ, D_MODEL), 1.0),
        'state_lru_h': nrm(ks[2], (DEPTH, DEC_BATCH, D_LRU), 0.5),
        'state_lru_conv': nrm(ks[3], (DEPTH, DEC_BATCH, LRU_CONV_W - 1, D_LRU), 1.0),
        'state_sc_conv': nrm(ks[4], (DEPTH, DEC_BATCH, SC_CONV_W - 1, D_SC), 1.0),
        'p_prompt': nrm(ks[5], (DEPTH, BATCH, SEQ, D_PLE), 1.0),
        'p_sample': nrm(ks[6], (DEPTH, DEC_BATCH, DEC_SEQ, D_PLE), 1.0),
        'w_in': nrm(ks[7], (DEPTH, D_MODEL, IN_WIDTH), D_MODEL ** -0.5),
        'lru_conv_w': nrm(ks[8], (DEPTH, LRU_CONV_W, D_LRU), LRU_CONV_W ** -0.5),
        'lru_conv_b': nrm(ks[9], (DEPTH, D_LRU), 0.01),
        'lru_wa': nrm(ks[10], (DEPTH, LRU_HEADS, LRU_HEAD_DIM, LRU_HEAD_DIM), LRU_HEAD_DIM ** -0.5),
        'lru_ba': nrm(ks[11], (DEPTH, LRU_HEADS, LRU_HEAD_DIM), 0.01),
        'lru_wx': nrm(ks[12], (DEPTH, LRU_HEADS, LRU_HEAD_DIM, LRU_HEAD_DIM), LRU_HEAD_DIM ** -0.5),
        'lru_bx': nrm(ks[13], (DEPTH, LRU_HEADS, LRU_HEAD_DIM), 0.01),
        'lru_lambda': jnp.log(a_base) - jnp.log1p(-a_base),
        'sc_conv_w': nrm(ks[15], (DEPTH, SC_CONV_W, D_SC), SC_CONV_W ** -0.5),
        'gn_lru': 1.0 + nrm(ks[16], (DEPTH, D_LRU), 0.01),
        'gn_sc': 1.0 + nrm(ks[17], (DEPTH, D_SC), 0.01),
        'w_out': nrm(ks[18], (DEPTH, D_MIX, D_MODEL), (D_MIX ** -0.5) * DEEPNORM_BETA),
        'ple_wp': nrm(ks[19], (DEPTH, D_PLE, D_MODEL), D_PLE ** -0.5),
        'ple_wg': nrm(ks[20], (DEPTH, D_MODEL, D_MODEL), D_MODEL ** -0.5),
        'ple_bg': nrm(ks[21], (DEPTH, D_MODEL), 0.01),
        'ln_g': 1.0 + nrm(ks[22], (DEPTH, D_MODEL), 0.01),
        'ln_b': nrm(ks[23], (DEPTH, D_MODEL), 0.01),
    }


def reference(x_prompt, x_sample, state_lru_h, state_lru_conv, state_sc_conv, p_prompt, p_sample,
              w_in, lru_conv_w, lru_conv_b, lru_wa, lru_ba, lru_wx, lru_bx, lru_lambda,
              sc_conv_w, gn_lru, gn_sc, w_out, ple_wp, ple_wg, ple_bg, ln_g, ln_b):
    dt = x_prompt.dtype
    h0_p = jnp.zeros((DEPTH, BATCH, D_LRU), dt)
    lb_p = jnp.zeros((DEPTH, BATCH, LRU_CONV_W - 1, D_LRU), dt)
    sb_p = jnp.zeros((DEPTH, BATCH, SC_CONV_W - 1, D_SC), dt)
    y_prompt, lru_h_prompt, lru_conv_prompt, sc_conv_prompt = run_trunk(
        x_prompt, p_prompt, h0_p, lb_p, sb_p, w_in, lru_conv_w, lru_conv_b, lru_wa, lru_ba,
        lru_wx, lru_bx, lru_lambda, sc_conv_w, gn_lru, gn_sc, w_out, ple_wp, ple_wg, ple_bg,
        ln_g, ln_b)
    y_sample, lru_h_sample, lru_conv_sample, sc_conv_sample = run_trunk(
        x_sample, p_sample, state_lru_h, state_lru_conv, state_sc_conv, w_in, lru_conv_w,
        lru_conv_b, lru_wa, lru_ba, lru_wx, lru_bx, lru_lambda, sc_conv_w, gn_lru, gn_sc,
        w_out, ple_wp, ple_wg, ple_bg, ln_g, ln_b)
    return (y_prompt, y_sample, lru_h_prompt, lru_conv_prompt, sc_conv_prompt,
            lru_h_sample, lru_conv_sample, sc_conv_sample)
```

```python
import os
import types
import numpy as np
import concourse.bass as bass
import concourse.mybir as mybir
from concourse.bass_utils import run_bass_kernel_spmd

F32 = mybir.dt.float32
BF16 = mybir.dt.bfloat16
AF = mybir.ActivationFunctionType
ALU = mybir.AluOpType

NCORES = 8
D = 2048
DL = 1024
NH = 8
SEQ = 2048
NSS = 16
DS = 4
TS = NSS * DS
DPLE = 256
DEPTH = 2
ALPHA = (2.0 * DEPTH) ** 0.25
LN_EPS = 1e-5
GN_EPS = 1e-6
TP = 1024
TMAX = TP + TS
CW, CB, BA, BX, LAM, SCW, GNL, GNS, BG, NCST = 0, 32, 40, 48, 56, 64, 88, 96, 104, 120
NL_COLS = 68
NS_COLS = 34
NWSLOT = 3
SAME_ENGINE_WAITS = os.environ.get("KSAME", "1") == "1"


def _freeze(fn):
    if fn.__closure__ is None:
        return fn
    cells = []
    for c in fn.__closure__:
        try:
            cells.append(types.CellType(c.cell_contents))
        except ValueError:
            cells.append(c)
    return types.FunctionType(fn.__code__, fn.__globals__, fn.__name__, fn.__defaults__, tuple(cells))


class Sched:
    def __init__(self):
        self.q = {}
        self.cnt = {}
        self.waited = {}
        self.lastw = {}
        self.readers = {}
        self.sems = set()

    def _eng(self, eng):
        if eng not in self.q:
            self.q[eng] = []
            self.waited[eng] = {}

    def _waits(self, eng, reads, writes, extra):
        self._eng(eng)
        need = {}
        evs = list(extra)
        for r in reads:
            ev = self.lastw.get(r)
            if ev:
                evs.append(ev)
        for w in writes:
            ev = self.lastw.get(w)
            if ev:
                evs.append(ev)
            evs.extend(self.readers.get(w, ()))
        for s, v in evs:
            if need.get(s, 0) < v:
                need[s] = v
        for s, v in need.items():
            if s == "e_" + eng and not SAME_ENGINE_WAITS:
                continue
            if self.waited[eng].get(s, 0) < v:
                self.q[eng].append(("wait", s, v))
                self.waited[eng][s] = v

    def _commit(self, ev, reads, writes):
        for r in reads:
            self.readers.setdefault(r, []).append(ev)
        for w in writes:
            self.lastw[w] = ev
            self.readers[w] = []

    def op(self, eng, fn, reads=(), writes=(), extra=()):
        self._waits(eng, reads, writes, extra)
        sem = "e_" + eng
        self.sems.add(sem)
        self.cnt[sem] = self.cnt.get(sem, 0) + 1
        ev = (sem, self.cnt[sem])
        self.q[eng].append(("op", _freeze(fn), sem, 1))
        self._commit(ev, reads, writes)
        return ev

    def dma(self, eng, fn, sem, reads=(), writes=(), extra=()):
        self._waits(eng, reads, writes, extra)
        self.sems.add(sem)
        self.cnt[sem] = self.cnt.get(sem, 0) + 16
        ev = (sem, self.cnt[sem])
        self.q[eng].append(("op", _freeze(fn), sem, 16))
        self._commit(ev, reads, writes)
        return ev

    def final_wait(self, eng, evs):
        self._waits(eng, (), (), evs)


def build_program():
    nc = bass.Bass("TRN2", target_bir_lowering=False)

    def din(name, shape):
        return nc.dram_tensor(name, list(shape), F32, kind="ExternalInput").ap()

    def dout(name, shape):
        return nc.dram_tensor(name, list(shape), F32, kind="ExternalOutput").ap()

    xp = din("xp", [SEQ, D])
    xs = din("xs", [TS, D])
    pp = din("pp", [DEPTH, SEQ, DPLE])
    ps = din("ps", [DEPTH, TS, DPLE])
    sin = din("sin", [DEPTH, 96, DL])
    win = din("win", [DEPTH, 48, 128, 2048])
    wout = din("wout", [DEPTH, 16, 128, 2048])
    wgt = din("wgt", [DEPTH, 16, 128, 2048])
    wpt = din("wpt", [DEPTH, 16, 128, 256])
    wat = din("wat", [DEPTH, 128, 1024])
    wxt = din("wxt", [DEPTH, 128, 1024])
    cst = din("cst", [DEPTH, 128, NCST])
    lng = din("lng", [DEPTH, D])
    lnb = din("lnb", [DEPTH, D])
    ident_d = din("ident", [128, 128])

    yp = dout("yp", [SEQ, D])
    ys = dout("ys", [TS, D])
    o_l = dout("o_l", [DEPTH, NL_COLS, DL])
    o_s = dout("o_s", [DEPTH, NS_COLS, DL])
    x1p = nc.dram_tensor("x1p", [SEQ, D], F32).ap()
    x1s = nc.dram_tensor("x1s", [TS, D], F32).ap()

    S = Sched()
    KST = int(os.environ.get("KSTAGE", "9"))
    from contextlib import ExitStack
    es = ExitStack()

    def sb(name, shape, dt=F32):
        return es.enter_context(nc.sbuf_tensor(name, list(shape), dt))

    XB = sb("XB", [128, 16, TMAX], BF16)
    R = sb("R", [128, 16, TMAX], F32)
    YR = sb("YR", [128, 16 * TMAX // 2], F32)
    Y = YR[:].bitcast(BF16).rearrange("p (c t) -> p c t", c=16)
    CW4 = TMAX * 2
    VT = [YR[:, 0:2048], YR[:, CW4:CW4 + 2048]]
    LGB = YR[:, 2 * CW4:2 * CW4 + 2048]
    LBB = YR[:, 3 * CW4:3 * CW4 + 2048]
    VTK = [["Y%d" % i for i in range(0, 4)], ["Y%d" % i for i in range(4, 8)]]
    LGK = ["Y%d" % i for i in range(8, 12)]
    LBK = ["Y%d" % i for i in range(12, 16)]
    PT = sb("PT", [128, 2, TMAX], BF16)
    STG = [sb("STG%d" % i, [128, 512]) for i in range(2)]
    PSTG = sb("PSTG", [128, DPLE])
    PSTG2 = sb("PSTG2", [128, DPLE])
    WR = [sb("WR%d" % i, [128, 16, 128], BF16) for i in range(NWSLOT)]
    WPS = [sb("WPS%d" % i, [128, 2, 128], BF16) for i in range(2)]
    WA = sb("WA", [128, 1024], BF16)
    WX = sb("WX", [128, 1024], BF16)
    CST = sb("CST", [128, DEPTH, NCST])
    CL = sb("CL", [128, DEPTH, 8])
    CL2 = sb("CL2", [128, DEPTH, 8])
    IDN = sb("IDN", [128, 128])
    ONESB = sb("ONESB", [128, 128], BF16)
    INST = sb("INST", [128, DEPTH, 8, 96])
    OUTL = sb("OUTL", [128, DEPTH, 8, NL_COLS])
    OUTS = sb("OUTS", [128, DEPTH, 8, NS_COLS])
    XL = sb("XL", [128, 3 + 512])
    XLS = sb("XLS", [128, NSS, 3 + DS])
    CHF = sb("CHF", [128, 2 + TP])
    CHS = sb("CHS", [128, NSS, 2 + DS])
    MISC = CHF[:, 0:1024]
    MISCK = ["CHh", "CHb0", "CHb1"]
    WK = sb("WK", [128, 6, 512])
    XC, RG, IG, SG, T1, HS = [WK[:, i, :] for i in range(6)]
    XCB = sb("XCB", [128, 512], BF16)
    SQ = sb("SQ", [128, 512], BF16)
    FENCE = sb("FENCE", [128, 2])
    SQB = sb("SQB", [128, 512], BF16)
    TT2 = sb("TT2", [128, 512])
    RF = R[:].rearrange("p a b -> p (a b)")
    WK1 = RF[:, 0:3072].rearrange("p (a b) -> p a b", a=6)
    XL1 = RF[:, 3072:3072 + 515]
    XCB1 = RF[:, 3588:3588 + 256].bitcast(BF16)
    SQ1 = RF[:, 3844:3844 + 256].bitcast(BF16)
    VF = RF[:, 4100:4100 + 1536]
    SGF = RF[:, 5636:5636 + 1536]
    SC_TT = [RF[:, 7172:7684], RF[:, 7684:8196], RF[:, 8708:8772]]
    SC_SQ = [RF[:, 8196:8452].bitcast(BF16), RF[:, 8452:8708].bitcast(BF16), RF[:, 8772:8804].bitcast(BF16)]
    WKS = sb("WKS", [128, 6, TS])
    XCBS = sb("XCBS", [128, TS], BF16)
    SQS = sb("SQS", [128, TS], BF16)
    BUFS = {
        0: dict(sfx="", ro=[], XL=XL, XC=XC, RG=RG, IG=IG, SG=SG, T1=T1, HS=HS, XCB=XCB, SQ=SQ),
        1: dict(sfx="1", ro=["RAL"], XL=XL1, XC=WK1[:, 0, :], RG=WK1[:, 1, :], IG=WK1[:, 2, :], SG=WK1[:, 3, :],
                T1=WK1[:, 4, :], HS=WK1[:, 5, :], XCB=XCB1, SQ=SQ1),
        "s": dict(sfx="s", ro=[], XL=XLS, XC=WKS[:, 0, :], RG=WKS[:, 1, :], IG=WKS[:, 2, :], SG=WKS[:, 3, :],
                  T1=WKS[:, 4, :], HS=WKS[:, 5, :], XCB=XCBS, SQ=SQS),
    }
    ST6 = sb("ST6", [128, 4, 6])
    MV = sb("MV", [128, 4])
    PSB = [es.enter_context(nc.psum_tensor("PS%d" % i, [128, 512], F32)) for i in range(8)]

    free_banks = {"p": list(range(6)), "s": [6, 7]}

    def bank(pool="p"):
        assert free_banks[pool], "no released PSUM bank in pool " + pool
        i = free_banks[pool].pop(0)
        return PSB[i], "ps%d" % i

    def rel(bkey):
        i = int(bkey[2:])
        pool = "p" if i < 6 else "s"
        assert i not in free_banks[pool]
        free_banks[pool].append(i)

    wctr = [0]

    def load_w(src):
        i = wctr[0] % NWSLOT
        wctr[0] += 1
        S.dma("pool", lambda e, i=i, src=src: e.dma_start(
            out=WR[i][:].rearrange("p a b -> p (a b)"), in_=src), "w%d" % i,
            writes=["w%d" % i])
        return WR[i], "w%d" % i

    VT3 = VT + [WK[:, 0:4, :].rearrange("p a b -> p (a b)")]
    VTK3 = VTK + [["XC", "RG", "IG", "SG"]]
    stg_ctr = [0]
    pstg_ctr = [0]
    STGS = [(STG[0], "stg0", "dstg0"), (STG[1], "stg1", "dstg1")] + [
        (WK[:, i, :], k, "dstg%d" % (i + 2)) for i, k in enumerate(["XC", "RG", "IG", "SG", "T1", "HS"])]
    STGS0 = [STGS[0], STGS[1], STGS[6], STGS[7],
             (CHF[:, 2:514], "CHb0", "dstg8"), (CHF[:, 514:1026], "CHb1", "dstg9")]
    PSTGS = [PSTG, PSTG2]

    S.dma("sp", lambda e: e.dma_start(out=IDN[:], in_=ident_d), "c0", writes=["IDN"])
    S.dma("sp", lambda e: e.dma_start(out=CST[:], in_=cst.rearrange("l p n -> p l n")), "c1",
          writes=["CST"])
    S.op("dve", lambda e: e.memset(ONESB[:], 1.0 / 128.0), writes=["ONESB"])
    S.op("dve", lambda e: e.memset(OUTL[:], 0.0), writes=["OUTL"])
    S.op("dve", lambda e: e.memset(OUTS[:], 0.0), writes=["OUTS"])
    S.op("act", lambda e: e.activation(out=CL[:], in_=CST[:, :, LAM:LAM + 8], func=AF.Exp, scale=-1.0),
         reads=["CST"], writes=["CL"])
    S.op("act", lambda e: e.activation(out=CL[:], in_=CL[:], func=AF.Ln, bias=1.0, scale=1.0),
         reads=["CL"], writes=["CL"])
    S.op("dve", lambda e: e.tensor_scalar(out=CL2[:], in0=CL[:], scalar1=-16.0, scalar2=None, op0=ALU.mult),
         reads=["CL"], writes=["CL2"])
    S.op("dve", lambda e: e.tensor_scalar(out=CL[:], in0=CL[:], scalar1=-8.0, scalar2=None, op0=ALU.mult),
         reads=["CL", "CL2"], writes=["CL"])
    for l in range(DEPTH):
        S.dma("sp", lambda e, l=l: e.dma_start(out=MISC[0:96, :], in_=sin[l]), "c2",
              writes=MISCK)
        for h in range(NH):
            bk, bkey = bank()
            S.op("pe", lambda e, h=h, bk=bk: e.transpose(
                out=bk[:, 0:96], in_=MISC[0:96, h * 128:(h + 1) * 128], identity=IDN[0:96, 0:96]),
                reads=MISCK + ["IDN"], writes=[bkey])
            S.op("act", lambda e, l=l, h=h, bk=bk: e.copy(out=INST[:, l, h, :], in_=bk[:, 0:96]),
                 reads=[bkey], writes=["INST"])
            rel(bkey)

    out_events = []

    def tile_rts(tl):
        return [(i * 128, 128, False) for i in range(8)] + ([(1024, TS, True)] if tl == 0 else [])

    def load_x_T(l, tl, c0, nr, is_s, to_R, slots, qs=(0, 1, 2, 3)):
        srcp, srcs = (xp, xs) if l == 0 else (x1p, x1s)
        r0 = tl * TP + c0
        for q in qs:
            si = stg_ctr[0] % len(slots)
            stg_ctr[0] += 1
            stg_t, stg_k, stg_sem = slots[si]
            src = srcs[0:nr, q * 512:(q + 1) * 512] if is_s else srcp[r0:r0 + nr, q * 512:(q + 1) * 512]
            rk = ["x1s" if is_s else "x1p%d" % (r0 // 128)] if l == 1 else []
            S.dma("sp", lambda e, stg_t=stg_t, src=src, nr=nr: e.dma_start(out=stg_t[0:nr, :], in_=src),
                  stg_sem, reads=rk, writes=[stg_k])
            bk, bkey = bank()

            def tr(e, stg_t=stg_t, bk=bk, nr=nr):
                ins = None
                for j in range(4):
                    ins = e.transpose(out=bk[:, j * 128:j * 128 + nr],
                                      in_=stg_t[0:nr, j * 128:(j + 1) * 128],
                                      identity=IDN[0:nr, 0:nr])
                return ins
            S.op("pe", tr, reads=[stg_k, "IDN"], writes=[bkey])
            pv = bk[:].rearrange("p (j t) -> p j t", j=4)[:, :, 0:nr]
            if to_R:
                S.op("act", lambda e, q=q, c0=c0, nr=nr, pv=pv: e.copy(
                    out=R[:, 4 * q:4 * q + 4, c0:c0 + nr], in_=pv),
                    reads=[bkey], writes=["R%d" % d for d in range(4 * q, 4 * q + 4)] + ["RAL"])
            else:
                S.op("act", lambda e, q=q, c0=c0, nr=nr, pv=pv: e.copy(
                    out=XB[:, 4 * q:4 * q + 4, c0:c0 + nr], in_=pv),
                    reads=[bkey], writes=["XB%d" % d for d in range(4 * q, 4 * q + 4)])
            rel(bkey)

    def phase0_rt(l, tl, ri):
        c0, nr, is_s = tile_rts(tl)[ri]
        r0 = tl * TP + c0
        load_x_T(l, tl, c0, nr, is_s, False, STGS0)
        psrc = ps[l, 0:nr, :] if is_s else pp[l, r0:r0 + nr, :]
        pi = pstg_ctr[0] % 2
        pstg_ctr[0] += 1
        pst = PSTGS[pi]
        S.dma("sp", lambda e, pst=pst, psrc=psrc, nr=nr: e.dma_start(out=pst[0:nr, :], in_=psrc), "pstg%d" % pi,
              writes=["pstg%d" % pi])
        bk, bkey = bank()

        def trp(e, pst=pst, bk=bk, nr=nr):
            ins = None
            for j in range(2):
                ins = e.transpose(out=bk[:, j * 128:j * 128 + nr], in_=pst[0:nr, j * 128:(j + 1) * 128],
                                  identity=IDN[0:nr, 0:nr])
            return ins
        S.op("pe", trp, reads=["pstg%d" % pi, "IDN"], writes=[bkey])
        pv = bk[:, 0:256].rearrange("p (j t) -> p j t", j=2)[:, :, 0:nr]
        S.op("dve", lambda e, c0=c0, nr=nr, pv=pv: e.tensor_copy(out=PT[:, :, c0:c0 + nr], in_=pv),
             reads=[bkey], writes=["PT"])
        rel(bkey)

    def run_tile(l, tl, first_tile, nxt_tile):
        has_s = (tl == 0)
        T = TP + (TS if has_s else 0)
        segs = [(0, 512, False), (512, 512, False)] + ([(1024, TS, True)] if has_s else [])
        rts = [(i * 128, 128, False) for i in range(8)] + ([(1024, TS, True)] if has_s else [])
        srcp, srcs = (xp, xs) if l == 0 else (x1p, x1s)
        dstp, dsts = (x1p, x1s) if l == 0 else (yp, ys)
        C = lambda off, n=1: CST[:, l, off:off + n]

        if True:
            S.dma("pool", lambda e: e.dma_start(out=WA[:], in_=wat[l]), "wa", writes=["WA"])
            S.dma("pool", lambda e: e.dma_start(out=WX[:], in_=wxt[l]), "wx", writes=["WX"])

        if first_tile:
            for ri0 in range(len(rts)):
                phase0_rt(l, tl, ri0)

        XBK = ["XB%d" % d for d in range(16)]

        def zmm(wt, wkey, rhs_of_kc, n, rkeys, nk=16, pool="p"):
            bk, bkey = bank(pool)

            def f(e, bk=bk):
                ins = None
                for kc in range(nk):
                    ins = e.matmul(bk[:, 0:n], lhsT=wt[:, kc, :], rhs=rhs_of_kc(kc),
                                   start=(kc == 0), stop=(kc == nk - 1))
                return ins
            S.op("pe", f, reads=[wkey] + rkeys, writes=[bkey])
            return bk, bkey

        def group_norm_store(ysrc, ykeys, n, gcol, ych, c0, TT=None, tkeys=("T1",)):
            TT = T1 if TT is None else TT
            tkeys = list(tkeys)
            ykeys = list(ykeys)
            S.op("act", lambda e: e.activation(out=SQ[:, 0:n], in_=ysrc, func=AF.Square),
                 reads=ykeys, writes=["SQ"])
            bk, bkey = bank()
            S.op("pe", lambda e, bk=bk: e.matmul(bk[:, 0:n], lhsT=ONESB[:], rhs=SQ[:, 0:n], start=True, stop=True),
                 reads=["SQ", "ONESB"], writes=[bkey])
            S.op("act", lambda e, bk=bk: e.activation(out=TT[:, 0:n], in_=bk[:, 0:n], func=AF.Ln,
                                                      bias=GN_EPS, scale=1.0),
                 reads=[bkey], writes=tkeys)
            rel(bkey)
            S.op("act", lambda e: e.activation(out=TT[:, 0:n], in_=TT[:, 0:n], func=AF.Exp, scale=-0.5),
                 reads=tkeys, writes=tkeys)
            S.op("dve", lambda e: e.scalar_tensor_tensor(
                out=Y[:, ych, c0:c0 + n], in0=ysrc, scalar=C(gcol), in1=TT[:, 0:n],
                op0=ALU.mult, op1=ALU.mult),
                reads=ykeys + tkeys + ["CST"], writes=["Y%d" % ych])

        S.op("dve", lambda e: e.memset(FENCE[:], 0.0), writes=["R%d" % d for d in range(16)] + ["RAL"])
        STR = []
        for si_, (c0, n, is_s) in enumerate(segs):
            B = dict(BUFS["s" if is_s else si_])
            B.update(c0=c0, n=n, is_s=is_s)
            STR.append(B)

        def kx(B, *names):
            return [nm + B["sfx"] for nm in names] + B["ro"]

        zb = {}

        def emit_Zx(h):
            wx_t, wx_k = load_w(win[l, h])
            for B in STR:
                rf = lambda kc, B=B: XB[:, kc, B["c0"]:B["c0"] + B["n"]]
                zb[(h, B["sfx"], "x")] = zmm(wx_t, wx_k, rf, B["n"], XBK, pool="s" if B["is_s"] else "p")

        def emit_Zg(h):
            wg_t, wg_k = load_w(win[l, 8 + h])
            for B in STR:
                rf = lambda kc, B=B: XB[:, kc, B["c0"]:B["c0"] + B["n"]]
                zb[(h, B["sfx"], "g")] = zmm(wg_t, wg_k, rf, B["n"], XBK, pool="s" if B["is_s"] else "p")

        def fx_halo(h):
            OL = OUTL[:, l, h, :]
            for B in STR:
                if B["is_s"]:
                    S.op("dve", lambda e, B=B: e.tensor_copy(
                        out=B["XL"][:, :, 0:3], in_=INST[:, l, h, 0:48].rearrange("p (s k) -> p s k", k=3)),
                        reads=["INST"] + B["ro"], writes=kx(B, "XLh"))
                elif B["c0"] == 0:
                    S.op("dve", lambda e, B=B, OL=OL: e.tensor_copy(out=B["XL"][:, 0:3], in_=OL[:, 1:4]),
                         reads=["OUTL"] + B["ro"], writes=kx(B, "XLh"))

        def fx_evac(h, B):
            bx, bxk = zb[(h, B["sfx"], "x")]
            n = B["n"]
            if B["is_s"]:
                bx3 = bx[:, 0:TS].rearrange("p (s t) -> p s t", t=DS)
                S.op("act", lambda e, B=B, bx3=bx3: e.copy(out=B["XL"][:, :, 3:3 + DS], in_=bx3),
                     reads=[bxk] + B["ro"], writes=kx(B, "XLb"))
            else:
                S.op("act", lambda e, B=B, bx=bx, n=n: e.copy(out=B["XL"][:, 3:3 + n], in_=bx[:, 0:n]),
                     reads=[bxk] + B["ro"], writes=kx(B, "XLb"))
            rel(bxk)

        def fx_conv(h, B):
            n = B["n"]
            if (not B["is_s"]) and B["c0"] > 0:
                B0 = STR[0]
                S.op("dve", lambda e, B0=B0, B=B: e.tensor_copy(out=B["XL"][:, 0:3],
                                                                in_=B0["XL"][:, B0["n"]:B0["n"] + 3]),
                     reads=kx(B0, "XLb") + B["ro"], writes=kx(B, "XLh"))
            if B["is_s"]:
                taps = [B["XL"][:, :, k:k + DS] for k in range(4)]
                xc = B["XC"][:, 0:n].rearrange("p (s t) -> p s t", t=DS)
            else:
                taps = [B["XL"][:, k:k + n] for k in range(4)]
                xc = B["XC"][:, 0:n]
            S.op("dve", lambda e, taps=taps, xc=xc: e.tensor_scalar(
                out=xc, in0=taps[3], scalar1=C(CW + h * 4 + 3), scalar2=C(CB + h),
                op0=ALU.mult, op1=ALU.add), reads=kx(B, "XLb", "XLh") + ["CST"], writes=kx(B, "XC"))
            for k in (2, 1, 0):
                S.op("dve", lambda e, taps=taps, xc=xc, k=k: e.scalar_tensor_tensor(
                    out=xc, in0=taps[k], scalar=C(CW + h * 4 + k), in1=xc, op0=ALU.mult, op1=ALU.add),
                    reads=kx(B, "XLb", "XLh", "XC") + ["CST"], writes=kx(B, "XC"))

        def fx_cast(h, B):
            n = B["n"]
            S.op("act", lambda e, B=B, n=n: e.copy(out=B["XCB"][:, 0:n], in_=B["XC"][:, 0:n]),
                 reads=kx(B, "XC"), writes=kx(B, "XCB"))

        def fx_finish(h):
            OL = OUTL[:, l, h, :]
            for B in STR:
                if B["is_s"]:
                    S.op("dve", lambda e, B=B, OL=OL: e.tensor_copy(
                        out=OL[:, 20:68].rearrange("p (s k) -> p s k", k=3), in_=B["XL"][:, :, 4:7]),
                        reads=kx(B, "XLb"), writes=["OUTL"])
            Bl = [B for B in STR if not B["is_s"]][-1]
            S.op("dve", lambda e, Bl=Bl, OL=OL: e.tensor_copy(out=OL[:, 1:4], in_=Bl["XL"][:, Bl["n"]:Bl["n"] + 3]),
                 reads=kx(Bl, "XLb"), writes=["OUTL"])

        def front_x(h):
            fx_halo(h)
            for B in STR:
                fx_evac(h, B)
            for B in STR:
                fx_conv(h, B)
                fx_cast(h, B)
            fx_finish(h)

        def gates(h):
            for B in STR:
                n = B["n"]
                if B["is_s"]:
                    for nm, W_, col, dst in (("r", WA, BA, "RG"), ("i", WX, BX, "IG")):
                        bk, bkey = bank("s")
                        S.op("pe", lambda e, B=B, bk=bk, n=n, W_=W_: e.matmul(
                            bk[:, 0:n], lhsT=W_[:, h * 128:(h + 1) * 128], rhs=B["XCB"][:, 0:n], start=True, stop=True),
                            reads=["WA", "WX"] + kx(B, "XCB"), writes=[bkey])
                        S.op("act", lambda e, B=B, bk=bk, n=n, col=col, dst=dst: e.activation(
                            out=B[dst][:, 0:n], in_=bk[:, 0:n], func=AF.Sigmoid, bias=C(col + h), scale=1.0),
                            reads=[bkey, "CST"] + B["ro"], writes=kx(B, dst))
                        rel(bkey)
                    continue
                br, brk = bank("p")
                S.op("pe", lambda e, B=B, br=br, n=n: e.matmul(br[:, 0:n], lhsT=WA[:, h * 128:(h + 1) * 128],
                                                               rhs=B["XCB"][:, 0:n], start=True, stop=True),
                     reads=["WA"] + kx(B, "XCB"), writes=[brk])
                bi, bik = bank("p")
                S.op("pe", lambda e, B=B, bi=bi, n=n: e.matmul(bi[:, 0:n], lhsT=WX[:, h * 128:(h + 1) * 128],
                                                               rhs=B["XCB"][:, 0:n], start=True, stop=True),
                     reads=["WX"] + kx(B, "XCB"), writes=[bik])
                zb[(h, B["sfx"], "r")] = (br, brk)
                zb[(h, B["sfx"], "i")] = (bi, bik)

        def sig_ri(h, B):
            br, brk = zb[(h, B["sfx"], "r")]
            bi, bik = zb[(h, B["sfx"], "i")]
            n = B["n"]
            S.op("act", lambda e, B=B, br=br, n=n: e.activation(out=B["RG"][:, 0:n], in_=br[:, 0:n],
                                                                func=AF.Sigmoid, bias=C(BA + h), scale=1.0),
                 reads=[brk, "CST"] + B["ro"], writes=kx(B, "RG"))
            S.op("act", lambda e, B=B, bi=bi, n=n: e.activation(out=B["IG"][:, 0:n], in_=bi[:, 0:n],
                                                                func=AF.Sigmoid, bias=C(BX + h), scale=1.0),
                 reads=[bik, "CST"] + B["ro"], writes=kx(B, "IG"))
            rel(brk)
            rel(bik)

        def front_g(h):
            for B in STR:
                bg, bgk = zb[(h, B["sfx"], "g")]
                n = B["n"]
                S.op("act", lambda e, B=B, bg=bg, n=n: e.activation(out=B["SG"][:, 0:n], in_=bg[:, 0:n],
                                                                    func=AF.Sigmoid),
                     reads=[bgk] + B["ro"], writes=kx(B, "SG"))
            for B in STR:
                bg, bgk = zb[(h, B["sfx"], "g")]
                n = B["n"]
                S.op("dve", lambda e, B=B, bg=bg, n=n: e.tensor_tensor(out=B["SG"][:, 0:n], in0=B["SG"][:, 0:n],
                                                                       in1=bg[:, 0:n], op=ALU.mult),
                     reads=[bgk] + kx(B, "SG"), writes=kx(B, "SG"))
                rel(bgk)

        def mid_a(h):
            for B in STR:
                if not B["is_s"]:
                    sig_ri(h, B)
            for B in STR:
                n = B["n"]
                S.op("dve", lambda e, B=B, n=n: e.tensor_tensor(out=B["IG"][:, 0:n], in0=B["IG"][:, 0:n],
                                                                in1=B["XC"][:, 0:n], op=ALU.mult),
                     reads=kx(B, "IG", "XC"), writes=kx(B, "IG"))

        def mid_chain(h, B):
            n = B["n"]
            S.op("act", lambda e, B=B, n=n: e.activation(out=B["RG"][:, 0:n], in_=B["RG"][:, 0:n], func=AF.Exp,
                                                         scale=CL[:, l, h:h + 1]),
                 reads=kx(B, "RG") + ["CL"], writes=kx(B, "RG"))
            S.op("act", lambda e, B=B, n=n: e.activation(out=B["T1"][:, 0:n], in_=B["RG"][:, 0:n], func=AF.Square),
                 reads=kx(B, "RG"), writes=kx(B, "T1"))
            S.op("act", lambda e, B=B, n=n: e.activation(out=B["T1"][:, 0:n], in_=B["T1"][:, 0:n], func=AF.Ln,
                                                         bias=1.0, scale=-1.0),
                 reads=kx(B, "T1"), writes=kx(B, "T1"))
            S.op("act", lambda e, B=B, n=n: e.activation(out=B["T1"][:, 0:n], in_=B["T1"][:, 0:n], func=AF.Exp,
                                                         scale=0.5),
                 reads=kx(B, "T1"), writes=kx(B, "T1"))

        def mid_tail(h, B):
            OL = OUTL[:, l, h, :]
            n = B["n"]
            S.op("dve", lambda e, B=B, n=n: e.tensor_tensor(out=B["IG"][:, 0:n], in0=B["IG"][:, 0:n],
                                                            in1=B["T1"][:, 0:n], op=ALU.mult),
                 reads=kx(B, "IG", "T1"), writes=kx(B, "IG"))
            if not B["is_s"]:
                S.op("dve", lambda e, B=B, n=n, OL=OL: e.tensor_tensor_scan(
                    out=B["HS"][:, 0:n], data0=B["RG"][:, 0:n], data1=B["IG"][:, 0:n], initial=OL[:, 0:1],
                    op0=ALU.mult, op1=ALU.add), reads=kx(B, "RG", "IG") + ["OUTL"], writes=kx(B, "HS"))
                S.op("dve", lambda e, B=B, n=n, OL=OL: e.tensor_copy(out=OL[:, 0:1], in_=B["HS"][:, n - 1:n]),
                     reads=kx(B, "HS"), writes=["OUTL"])
            else:
                a3 = B["RG"][:, 0:n].rearrange("p (s t) -> p s t", t=DS)
                u3 = B["IG"][:, 0:n].rearrange("p (s t) -> p s t", t=DS)
                h3 = B["HS"][:, 0:n].rearrange("p (s t) -> p s t", t=DS)
                S.op("dve", lambda e, a3=a3, h3=h3: e.tensor_tensor(
                    out=h3[:, :, 0], in0=a3[:, :, 0], in1=INST[:, l, h, 48:64], op=ALU.mult),
                    reads=kx(B, "RG", "HS") + ["INST"], writes=kx(B, "HS"))
                S.op("dve", lambda e, u3=u3, h3=h3: e.tensor_tensor(
                    out=u3[:, :, 0], in0=u3[:, :, 0], in1=h3[:, :, 0], op=ALU.add),
                    reads=kx(B, "IG", "HS"), writes=kx(B, "IG"))
                S.op("dve", lambda e, a3=a3: e.memset(a3[:, :, 0], 0.0),
                     reads=kx(B, "RG", "HS"), writes=kx(B, "RG"))
                S.op("dve", lambda e, B=B, n=n: e.tensor_tensor_scan(
                    out=B["HS"][:, 0:n], data0=B["RG"][:, 0:n], data1=B["IG"][:, 0:n], initial=0.0,
                    op0=ALU.mult, op1=ALU.add), reads=kx(B, "RG", "IG", "HS"), writes=kx(B, "HS"))
                S.op("dve", lambda e, OL=OL, h3=h3: e.tensor_copy(out=OL[:, 4:20], in_=h3[:, :, DS - 1]),
                     reads=kx(B, "HS"), writes=["OUTL"])
            S.op("dve", lambda e, B=B, n=n: e.tensor_tensor(out=B["HS"][:, 0:n], in0=B["HS"][:, 0:n],
                                                            in1=B["SG"][:, 0:n], op=ALU.mult),
                 reads=kx(B, "HS", "SG"), writes=kx(B, "HS"))

        def mid_sq(h, B):
            n = B["n"]
            S.op("act", lambda e, B=B, n=n: e.activation(out=B["SQ"][:, 0:n], in_=B["HS"][:, 0:n], func=AF.Square),
                 reads=kx(B, "HS"), writes=kx(B, "SQ"))

        def gn_mm(h):
            for B in STR:
                n = B["n"]
                bk, bkey = bank("s" if B["is_s"] else "p")
                S.op("pe", lambda e, B=B, bk=bk, n=n: e.matmul(bk[:, 0:n], lhsT=ONESB[:], rhs=B["SQ"][:, 0:n],
                                                               start=True, stop=True),
                     reads=kx(B, "SQ") + ["ONESB"], writes=[bkey])
                S.op("act", lambda e, B=B, bk=bk, n=n: e.copy(out=B["T1"][:, 0:n], in_=bk[:, 0:n]),
                     reads=[bkey] + B["ro"], writes=kx(B, "T1"))
                rel(bkey)

        def back(h):
            for B in STR:
                n = B["n"]
                S.op("act", lambda e, B=B, n=n: e.activation(out=B["T1"][:, 0:n], in_=B["T1"][:, 0:n], func=AF.Ln,
                                                             bias=GN_EPS, scale=1.0),
                     reads=kx(B, "T1"), writes=kx(B, "T1"))
            for B in STR:
                n = B["n"]
                S.op("act", lambda e, B=B, n=n: e.activation(out=B["T1"][:, 0:n], in_=B["T1"][:, 0:n], func=AF.Exp,
                                                             scale=-0.5),
                     reads=kx(B, "T1"), writes=kx(B, "T1"))
            for B in STR:
                n = B["n"]
                S.op("dve", lambda e, B=B, n=n: e.scalar_tensor_tensor(
                    out=Y[:, h, B["c0"]:B["c0"] + n], in0=B["HS"][:, 0:n], scalar=C(GNL + h), in1=B["T1"][:, 0:n],
                    op0=ALU.mult, op1=ALU.mult),
                    reads=kx(B, "HS", "T1") + ["CST"], writes=["Y%d" % h])

        YK = ["Y%d" % d for d in range(16)]
        RAL = ["RAL"]
        vk = lambda c0: ["VF%d" % (c0 // 512)] + RAL
        sk = lambda c0: ["SGF%d" % (c0 // 512)] + RAL
        VKA = ["VF0", "VF1", "VF2"] + RAL
        SQL = [(SC_SQ[i], "SCQ%d" % i) for i in range(3)]
        TTL = [(SC_TT[i], ["SCT%d" % i]) for i in range(3)]

        def sc_gn(g):
            for si_, (c0, n, is_s) in enumerate(segs):
                sq, sqk = SQL[si_]
                bk, bkey = bank("s" if is_s else "p")
                S.op("pe", lambda e, bk=bk, sq=sq, n=n: e.matmul(bk[:, 0:n], lhsT=ONESB[:], rhs=sq[:, 0:n],
                                                                 start=True, stop=True),
                     reads=[sqk, "ONESB"] + RAL, writes=[bkey])
                tt, tk = TTL[si_]
                S.op("act", lambda e, bk=bk, tt=tt, n=n: e.copy(out=tt[:, 0:n], in_=bk[:, 0:n]),
                     reads=[bkey] + RAL, writes=tk)
                rel(bkey)

        def sc_fin(g):
            for si_, (c0, n, is_s) in enumerate(segs):
                tt, tk = TTL[si_]
                S.op("act", lambda e, tt=tt, n=n: e.activation(out=tt[:, 0:n], in_=tt[:, 0:n], func=AF.Ln,
                                                               bias=GN_EPS, scale=1.0),
                     reads=tk + RAL, writes=tk)
            for si_, (c0, n, is_s) in enumerate(segs):
                tt, tk = TTL[si_]
                S.op("act", lambda e, tt=tt, n=n: e.activation(out=tt[:, 0:n], in_=tt[:, 0:n], func=AF.Exp, scale=-0.5),
                     reads=tk + RAL, writes=tk)
            for si_, (c0, n, is_s) in enumerate(segs):
                tt, tk = TTL[si_]
                S.op("dve", lambda e, tt=tt, c0=c0, n=n, g=g: e.scalar_tensor_tensor(
                    out=Y[:, 8 + g, c0:c0 + n], in0=VF[:, c0:c0 + n], scalar=C(GNS + g), in1=tt[:, 0:n],
                    op0=ALU.mult, op1=ALU.mult),
                    reads=vk(c0) + tk + ["CST"], writes=["Y%d" % (8 + g)])

        def sc_part(g, part):
            OSg = OUTS[:, l, g, :]
            if part == 0:
                S.op("dve", lambda e, OSg=OSg: e.tensor_copy(out=CHF[:, 0:2], in_=OSg[:, 0:2]),
                     reads=["OUTS"], writes=["CHh"])
                if has_s:
                    S.op("dve", lambda e, g=g: e.tensor_copy(
                        out=CHS[:, :, 0:2], in_=INST[:, l, g, 64:96].rearrange("p (s k) -> p s k", k=2)),
                        reads=["INST"], writes=["CHSh"])
                wt, wk = load_w(win[l, 24 + g])
                for (c0, n, is_s) in segs:
                    bk, bkey = zmm(wt, wk, lambda kc, c0=c0, n=n: XB[:, kc, c0:c0 + n], n, XBK,
                                   pool="s" if is_s else "p")
                    if not is_s:
                        S.op("act", lambda e, bk=bk, c0=c0, n=n: e.copy(out=CHF[:, 2 + c0:2 + c0 + n], in_=bk[:, 0:n]),
                             reads=[bkey], writes=["CHb%d" % (c0 // 512)])
                    else:
                        S.op("act", lambda e, bk=bk: e.copy(
                            out=CHS[:, :, 2:2 + DS], in_=bk[:, 0:TS].rearrange("p (s t) -> p s t", t=DS)),
                            reads=[bkey], writes=["CHSb"])
                    rel(bkey)
            elif part == 1:
                wt, wk = load_w(win[l, 32 + g])
                for (c0, n, is_s) in segs:
                    bk, bkey = zmm(wt, wk, lambda kc, c0=c0, n=n: XB[:, kc, c0:c0 + n], n, XBK,
                                   pool="s" if is_s else "p")
                    if not is_s:
                        S.op("dve", lambda e, bk=bk, c0=c0, n=n: e.tensor_tensor(
                            out=CHF[:, 2 + c0:2 + c0 + n], in0=CHF[:, 2 + c0:2 + c0 + n], in1=bk[:, 0:n], op=ALU.mult),
                            reads=[bkey, "CHb%d" % (c0 // 512)], writes=["CHb%d" % (c0 // 512)])
                    else:
                        S.op("dve", lambda e, bk=bk: e.tensor_tensor(
                            out=CHS[:, :, 2:2 + DS], in0=CHS[:, :, 2:2 + DS],
                            in1=bk[:, 0:TS].rearrange("p (s t) -> p s t", t=DS), op=ALU.mult),
                            reads=[bkey, "CHSb"], writes=["CHSb"])
                    rel(bkey)

                def conv3(taps, out, rk, wkeys):
                    S.op("dve", lambda e: e.tensor_scalar(out=out, in0=taps[2], scalar1=C(SCW + g * 3 + 2),
                                                          scalar2=None, op0=ALU.mult),
                         reads=rk + ["CST"] + RAL, writes=wkeys)
                    for k in (1, 0):
                        S.op("dve", lambda e, k=k: e.scalar_tensor_tensor(
                            out=out, in0=taps[k], scalar=C(SCW + g * 3 + k), in1=out, op0=ALU.mult, op1=ALU.add),
                            reads=rk + ["CST"] + RAL + wkeys, writes=wkeys)
                conv3([CHF[:, k:k + TP] for k in range(3)], VF[:, 0:TP], ["CHh", "CHb0", "CHb1"], ["VF0", "VF1"])
                S.op("dve", lambda e, OSg=OSg: e.tensor_copy(out=OSg[:, 0:2], in_=CHF[:, TP:TP + 2]),
                     reads=["CHb1"], writes=["OUTS"])
                if has_s:
                    conv3([CHS[:, :, k:k + DS] for k in range(3)],
                          VF[:, TP:TP + TS].rearrange("p (s t) -> p s t", t=DS), ["CHSh", "CHSb"], ["VF2"])
                    S.op("dve", lambda e, OSg=OSg: e.tensor_copy(
                        out=OSg[:, 2:34].rearrange("p (s k) -> p s k", k=2), in_=CHS[:, :, 4:6]),
                        reads=["CHSb"], writes=["OUTS"])
            elif part == 2:
                wt, wk = load_w(win[l, 40 + g])
                for (c0, n, is_s) in segs:
                    bk, bkey = zmm(wt, wk, lambda kc, c0=c0, n=n: XB[:, kc, c0:c0 + n], n, XBK,
                                   pool="s" if is_s else "p")
                    S.op("act", lambda e, bk=bk, c0=c0, n=n: e.activation(out=SGF[:, c0:c0 + n], in_=bk[:, 0:n],
                                                                          func=AF.Sigmoid),
                         reads=[bkey] + RAL, writes=sk(c0)[:1])
                    S.op("dve", lambda e, bk=bk, c0=c0, n=n: e.tensor_tensor(
                        out=SGF[:, c0:c0 + n], in0=SGF[:, c0:c0 + n], in1=bk[:, 0:n], op=ALU.mult),
                        reads=[bkey] + sk(c0), writes=sk(c0)[:1])
                    rel(bkey)
            else:
                wt, wk = load_w(win[l, 16 + g])
                for si_, (c0, n, is_s) in enumerate(segs):
                    bk, bkey = zmm(wt, wk, lambda kc, c0=c0, n=n: XB[:, kc, c0:c0 + n], n, XBK,
                                   pool="s" if is_s else "p")
                    S.op("dve", lambda e, bk=bk, c0=c0, n=n: e.tensor_tensor(
                        out=VF[:, c0:c0 + n], in0=VF[:, c0:c0 + n], in1=bk[:, 0:n], op=ALU.mult),
                        reads=[bkey] + vk(c0), writes=vk(c0)[:1])
                    rel(bkey)
                    S.op("dve", lambda e, c0=c0, n=n: e.tensor_tensor(
                        out=VF[:, c0:c0 + n], in0=VF[:, c0:c0 + n], in1=SGF[:, c0:c0 + n], op=ALU.mult),
                        reads=vk(c0) + sk(c0), writes=vk(c0)[:1])
                    sq, sqk = SQL[si_]
                    S.op("act", lambda e, sq=sq, c0=c0, n=n: e.activation(out=sq[:, 0:n], in_=VF[:, c0:c0 + n],
                                                                          func=AF.Square),
                         reads=vk(c0), writes=[sqk])
                pass

        def lru_sc_pipeline():
            emit_Zx(0)
            emit_Zg(0)
            front_x(0)
            front_g(0)
            gates(0)
            if NH > 1:
                emit_Zx(1)
            for h in range(NH):
                nxt = h + 1 < NH
                mid_a(h)
                if h >= 1:
                    sc_gn(h - 1)
                if nxt:
                    emit_Zg(h + 1)
                if h >= 1:
                    back(h - 1)
                for B in STR:
                    mid_chain(h, B)
                if h >= 1:
                    sc_fin(h - 1)
                if nxt:
                    fx_halo(h + 1)
                    for B in STR:
                        fx_evac(h + 1, B)
                sc_next = 0
                for B in STR:
                    mid_tail(h, B)
                    mid_sq(h, B)
                    if nxt:
                        fx_conv(h + 1, B)
                        fx_cast(h + 1, B)
                    if sc_next < 2:
                        sc_part(h, sc_next)
                        sc_next += 1
                if nxt:
                    fx_finish(h + 1)
                while sc_next < 2:
                    sc_part(h, sc_next)
                    sc_next += 1
                if nxt:
                    front_g(h + 1)
                sc_part(h, 2)
                sc_part(h, 3)
                gn_mm(h)
                if nxt:
                    gates(h + 1)
                if h + 2 < NH:
                    emit_Zx(h + 2)
            sc_gn(NH - 1)
            back(NH - 1)
            sc_fin(NH - 1)

        if KST >= 4:
            lru_sc_pipeline()

        for q4 in range(4):
            for (c0, nr, is_s) in (rts if KST >= 2 else []):
                load_x_T(l, tl, c0, nr, is_s, True, STGS, qs=(q4,))
            if KST < 5:
                continue
            for d in range(4 * q4, 4 * q4 + 4):
                wt, wk = load_w(wout[l, d])
                for (c0, n, is_s) in segs:
                    bk, bkey = zmm(wt, wk, lambda kc, c0=c0, n=n: Y[:, kc, c0:c0 + n], n, YK)
                    S.op("dve", lambda e, bk=bk, d=d, c0=c0, n=n: e.scalar_tensor_tensor(
                        out=R[:, d, c0:c0 + n], in0=R[:, d, c0:c0 + n], scalar=float(ALPHA), in1=bk[:, 0:n],
                        op0=ALU.mult, op1=ALU.add), reads=[bkey, "R%d" % d], writes=["R%d" % d])
                    rel(bkey)
                    S.op("act", lambda e, d=d, c0=c0, n=n: e.copy(out=XB[:, d, c0:c0 + n], in_=R[:, d, c0:c0 + n]),
                         reads=["R%d" % d], writes=["XB%d" % d])


        for d in range(16 if KST >= 6 else 0):
            wt, wk = load_w(wgt[l, d])
            wi = d % 2
            S.dma("pool", lambda e, wi=wi, d=d: e.dma_start(
                out=WPS[wi][:].rearrange("p a b -> p (a b)"), in_=wpt[l, d]), "wp%d" % wi, writes=["wp%d" % wi])
            for (c0, n, is_s) in segs:
                bk, bkey = zmm(wt, wk, lambda kc, c0=c0, n=n: XB[:, kc, c0:c0 + n], n, XBK)
                be, bekey = zmm(WPS[wi], "wp%d" % wi, lambda kc, c0=c0, n=n: PT[:, kc, c0:c0 + n], n, ["PT"], nk=2)
                S.op("act", lambda e, bk=bk, d=d, n=n: e.activation(out=T1[:, 0:n], in_=bk[:, 0:n], func=AF.Sigmoid,
                                                                    bias=C(BG + d), scale=1.0),
                     reads=[bkey, "CST"], writes=["T1"])
                S.op("dve", lambda e, be=be, n=n: e.tensor_tensor(out=T1[:, 0:n], in0=T1[:, 0:n], in1=be[:, 0:n],
                                                                  op=ALU.mult), reads=[bekey, "T1"], writes=["T1"])
                rel(bkey)
                rel(bekey)
                S.op("dve", lambda e, d=d, c0=c0, n=n: e.tensor_tensor(
                    out=R[:, d, c0:c0 + n], in0=R[:, d, c0:c0 + n], in1=T1[:, 0:n], op=ALU.add),
                    reads=["T1", "R%d" % d], writes=["R%d" % d])

        RK = ["R%d" % d for d in range(16)]
        S.dma("sp", lambda e: e.dma_start(out=LGB, in_=lng[l:l + 1, :].partition_broadcast(128)), "c3",
              reads=[], writes=LGK)
        S.dma("sp", lambda e: e.dma_start(out=LBB, in_=lnb[l:l + 1, :].partition_broadcast(128)), "c3b",
              reads=[], writes=LBK)
        rts4 = rts if (KST >= 2 and os.environ.get('KP4', '1') == '1') else []

        def p4_setup(ri):
            c0, nr, is_s = rts4[ri]
            return c0, nr, is_s, tl * TP + c0, VT3[ri % 3], VTK3[ri % 3]

        def p4_stage_a(ri):
            c0, nr, is_s, r0, vt, vk = p4_setup(ri)
            for q in range(4):
                bk, bkey = bank()

                def tr(e, bk=bk, q=q, c0=c0, nr=nr):
                    ins = None
                    for j in range(4):
                        ins = e.transpose(out=bk[0:nr, j * 128:(j + 1) * 128], in_=R[:, 4 * q + j, c0:c0 + nr],
                                          identity=IDN[:, :])
                    return ins
                S.op("pe", tr, reads=RK[4 * q:4 * q + 4] + ["IDN"], writes=[bkey])
                S.op("act", lambda e, bk=bk, q=q, nr=nr, vt=vt: e.copy(out=vt[0:nr, q * 512:(q + 1) * 512],
                                                                       in_=bk[0:nr, :]),
                     reads=[bkey], writes=vk)
                rel(bkey)

        def p4_stage_b(ri):
            c0, nr, is_s, r0, vt, vk = p4_setup(ri)
            for q in range(4):
                S.op("dve", lambda e, q=q, nr=nr, vt=vt: e.bn_stats(out=ST6[0:nr, q, :],
                                                                    in_=vt[0:nr, q * 512:(q + 1) * 512]),
                     reads=vk, writes=["ST6"])
            S.op("dve", lambda e, nr=nr: e.bn_aggr(out=MV[0:nr, 0:2], in_=ST6[0:nr, :, :]),
                 reads=["ST6"], writes=["MV"])
            S.op("act", lambda e, nr=nr: e.activation(out=MV[0:nr, 2:3], in_=MV[0:nr, 1:2], func=AF.Ln,
                                                      bias=LN_EPS, scale=1.0), reads=["MV"], writes=["MV2"])
            S.op("act", lambda e, nr=nr: e.activation(out=MV[0:nr, 2:3], in_=MV[0:nr, 2:3], func=AF.Exp,
                                                      scale=-0.5), reads=["MV2"], writes=["MV2"])
            S.op("dve", lambda e, nr=nr: e.scalar_tensor_tensor(
                out=MV[0:nr, 3:4], in0=MV[0:nr, 0:1], scalar=-1.0, in1=MV[0:nr, 2:3], op0=ALU.mult, op1=ALU.mult),
                reads=["MV", "MV2"], writes=["MV3"])
            S.op("act", lambda e, nr=nr, vt=vt: e.activation(out=vt[0:nr, :], in_=vt[0:nr, :], func=AF.Identity,
                                                             bias=MV[0:nr, 3:4], scale=MV[0:nr, 2:3]),
                 reads=vk + ["MV2", "MV3"], writes=vk)
            S.op("dve", lambda e, nr=nr, vt=vt: e.tensor_tensor(out=vt[0:nr, :], in0=vt[0:nr, :], in1=LGB[0:nr, :],
                                                                op=ALU.mult),
                 reads=vk + LGK, writes=vk)
            S.op("pool", lambda e, nr=nr, vt=vt: e.tensor_tensor(out=vt[0:nr, :], in0=vt[0:nr, :], in1=LBB[0:nr, :],
                                                                 op=ALU.add),
                 reads=vk + LBK, writes=vk)
            dst = dsts[0:nr, :] if is_s else dstp[r0:r0 + nr, :]
            wkeys = [("x1s" if is_s else "x1p%d" % (r0 // 128))] if l == 0 else []
            ev = S.dma("pool", lambda e, dst=dst, nr=nr, vt=vt: e.dma_start(out=dst, in_=vt[0:nr, :]),
                       "o%d" % (ri % 3), reads=vk, writes=wkeys)
            if l == DEPTH - 1:
                out_events.append(ev)


        n_next = len(tile_rts(nxt_tile[1])) if nxt_tile is not None else 0
        for ri in range(max(len(rts4) + 1, n_next)):
            if ri < len(rts4):
                p4_stage_a(ri)
            if 1 <= ri <= len(rts4):
                p4_stage_b(ri - 1)
            if ri < n_next:
                phase0_rt(nxt_tile[0], nxt_tile[1], ri)

    def layer_end(l):
        OT = MISC
        for (SRC, ncol, dstt, skey) in ((OUTL, NL_COLS, o_l, "OUTL"), (OUTS, NS_COLS, o_s, "OUTS")):
            for h in range(NH):
                bk, bkey = bank()
                S.op("pe", lambda e, bk=bk, h=h, SRC=SRC, ncol=ncol: e.transpose(
                    out=bk[0:ncol, 0:128], in_=SRC[:, l, h, :], identity=IDN[:, :]),
                    reads=[skey, "IDN"], writes=[bkey])
                S.op("act", lambda e, bk=bk, h=h, ncol=ncol: e.copy(out=OT[0:ncol, h * 128:(h + 1) * 128],
                                                                    in_=bk[0:ncol, 0:128]),
                     reads=[bkey], writes=MISCK)
                rel(bkey)
            ev = S.dma("sp", lambda e, dstt=dstt, ncol=ncol: e.dma_start(out=dstt[l], in_=OT[0:ncol, :]),
                       "c4", reads=MISCK, writes=[])
            out_events.append(ev)

    for l in range(DEPTH):
        for tl in range(2):
            order = [(a, b) for a in range(DEPTH) for b in range(2)]
            idx_ = order.index((l, tl))
            run_tile(l, tl, idx_ == 0, order[idx_ + 1] if idx_ + 1 < len(order) else None)
        layer_end(l)
    S.final_wait("sp", out_events)
    S.final_wait("pool", out_events)

    sem_names = sorted(S.sems)
    sems = {n: es.enter_context(nc.semaphore(n)) for n in sem_names}
    block = es.enter_context(nc.Block())

    def replay(eng_handle, items):
        for it in items:
            if it[0] == "wait":
                eng_handle.wait_ge(sems[it[1]], it[2])
            else:
                _, fn, sem, inc = it
                fn(eng_handle).then_inc(sems[sem], inc)

    @block.tensor
    def _(e):
        replay(e, S.q.get("pe", []))

    @block.scalar
    def _(e):
        replay(e, S.q.get("act", []))

    @block.vector
    def _(e):
        replay(e, S.q.get("dve", []))

    @block.gpsimd
    def _(e):
        replay(e, S.q.get("pool", []))

    @block.sync
    def _(e):
        replay(e, S.q.get("sp", []))

    es.close()
    return nc


_PROG = None


def _relayout_w(w):
    K, N = w.shape
    return np.ascontiguousarray(w.reshape(K // 128, 128, N // 128, 128).transpose(2, 1, 0, 3)).reshape(
        N // 128, 128, (K // 128) * 128)


def kernel(x_prompt, x_sample, state_lru_h, state_lru_conv, state_sc_conv, p_prompt, p_sample,
           w_in, lru_conv_w, lru_conv_b, lru_wa, lru_ba, lru_wx, lru_bx, lru_lambda,
           sc_conv_w, gn_lru, gn_sc, w_out, ple_wp, ple_wg, ple_bg, ln_g, ln_b):
    in_maps, nb = _prep(x_prompt, x_sample, state_lru_h, state_lru_conv, state_sc_conv, p_prompt, p_sample,
                        w_in, lru_conv_w, lru_conv_b, lru_wa, lru_ba, lru_wx, lru_bx, lru_lambda,
                        sc_conv_w, gn_lru, gn_sc, w_out, ple_wp, ple_wg, ple_bg, ln_g, ln_b)
    global _PROG
    if _PROG is None:
        _PROG = build_program()
    res = run_bass_kernel_spmd(_PROG, in_maps, core_ids=list(range(NCORES))).results
    return _assemble(res, nb)


def _prep(x_prompt, x_sample, state_lru_h, state_lru_conv, state_sc_conv, p_prompt, p_sample,
          w_in, lru_conv_w, lru_conv_b, lru_wa, lru_ba, lru_wx, lru_bx, lru_lambda,
          sc_conv_w, gn_lru, gn_sc, w_out, ple_wp, ple_wg, ple_bg, ln_g, ln_b):
    f = lambda a: np.asarray(a, dtype=np.float32)
    x_prompt, x_sample, p_prompt, p_sample = f(x_prompt), f(x_sample), f(p_prompt), f(p_sample)
    state_lru_h, state_lru_conv, state_sc_conv = f(state_lru_h), f(state_lru_conv), f(state_sc_conv)
    nb = x_prompt.shape[0]
    win = np.stack([_relayout_w(f(w_in[l])) for l in range(DEPTH)])
    wout = np.stack([_relayout_w(f(w_out[l])) for l in range(DEPTH)])
    wgt = np.stack([_relayout_w(f(ple_wg[l])) for l in range(DEPTH)])
    wpt = np.stack([_relayout_w(f(ple_wp[l])) for l in range(DEPTH)])
    wat = np.ascontiguousarray(f(lru_wa).transpose(0, 2, 1, 3)).reshape(DEPTH, 128, 1024)
    wxt = np.ascontiguousarray(f(lru_wx).transpose(0, 2, 1, 3)).reshape(DEPTH, 128, 1024)
    cst = np.zeros((DEPTH, 128, NCST), np.float32)
    fm = lambda v: f(v).reshape(DEPTH, -1, 128).transpose(0, 2, 1)
    cst[:, :, CW:CW + 32] = f(lru_conv_w).reshape(DEPTH, 4, 8, 128).transpose(0, 3, 2, 1).reshape(DEPTH, 128, 32)
    cst[:, :, CB:CB + 8] = fm(lru_conv_b)
    cst[:, :, BA:BA + 8] = f(lru_ba).transpose(0, 2, 1)
    cst[:, :, BX:BX + 8] = f(lru_bx).transpose(0, 2, 1)
    cst[:, :, LAM:LAM + 8] = fm(lru_lambda)
    cst[:, :, SCW:SCW + 24] = f(sc_conv_w).reshape(DEPTH, 3, 8, 128).transpose(0, 3, 2, 1).reshape(DEPTH, 128, 24)
    cst[:, :, GNL:GNL + 8] = fm(gn_lru)
    cst[:, :, GNS:GNS + 8] = fm(gn_sc)
    cst[:, :, BG:BG + 16] = fm(ple_bg)
    lng, lnb = f(ln_g), f(ln_b)
    ident = np.eye(128, dtype=np.float32)
    zx = np.zeros((SEQ, D), np.float32)
    zp = np.zeros((DEPTH, SEQ, DPLE), np.float32)
    in_maps = []
    for c in range(NCORES):
        sl = slice(NSS * c, NSS * (c + 1))
        sin = np.concatenate([state_lru_conv[:, sl].reshape(DEPTH, NSS * 3, DL),
                              state_lru_h[:, sl],
                              state_sc_conv[:, sl].reshape(DEPTH, NSS * 2, DL)], axis=1)
        in_maps.append({
            "xp": x_prompt[c] if c < nb else zx,
            "xs": x_sample[sl].reshape(TS, D),
            "pp": np.ascontiguousarray(p_prompt[:, c]) if c < nb else zp,
            "ps": np.ascontiguousarray(p_sample[:, sl].reshape(DEPTH, TS, DPLE)),
            "sin": np.ascontiguousarray(sin),
            "win": win, "wout": wout, "wgt": wgt, "wpt": wpt, "wat": wat, "wxt": wxt,
            "cst": cst, "lng": lng, "lnb": lnb, "ident": ident,
        })
    return in_maps, nb


def _assemble(res, nb):
    y_prompt = np.stack([res[c]["yp"] for c in range(nb)])
    y_sample = np.concatenate([res[c]["ys"].reshape(NSS, DS, D) for c in range(NCORES)], axis=0)
    ol = [res[c]["o_l"] for c in range(NCORES)]
    os_ = [res[c]["o_s"] for c in range(NCORES)]
    lru_h_prompt = np.stack([ol[c][:, 0] for c in range(nb)], axis=1)
    lru_conv_prompt = np.stack([ol[c][:, 1:4] for c in range(nb)], axis=1)
    sc_conv_prompt = np.stack([os_[c][:, 0:2] for c in range(nb)], axis=1)
    lru_h_sample = np.concatenate([ol[c][:, 4:20] for c in range(NCORES)], axis=1)
    lru_conv_sample = np.concatenate([ol[c][:, 20:68].reshape(DEPTH, NSS, 3, DL) for c in range(NCORES)], axis=1)
    sc_conv_sample = np.concatenate([os_[c][:, 2:34].reshape(DEPTH, NSS, 2, DL) for c in range(NCORES)], axis=1)
    return (y_prompt.astype(np.float32), y_sample.astype(np.float32), lru_h_prompt, lru_conv_prompt,
            sc_conv_prompt, lru_h_sample, lru_conv_sample, sc_conv_sample)
```

```python
import os
import types
import numpy as np
import concourse.bass as bass
import concourse.mybir as mybir
from concourse.bass_utils import run_bass_kernel_spmd

F32 = mybir.dt.float32
BF16 = mybir.dt.bfloat16
AF = mybir.ActivationFunctionType
ALU = mybir.AluOpType

NCORES = 8
D = 2048
DL = 1024
NH = 8
SEQ = 2048
NSS = 16
DS = 4
TS = NSS * DS
DPLE = 256
DEPTH = 2
ALPHA = (2.0 * DEPTH) ** 0.25
LN_EPS = 1e-5
GN_EPS = 1e-6
TP = 1024
TMAX = TP + TS
CW, CB, BA, BX, LAM, SCW, GNL, GNS, BG, NCST = 0, 32, 40, 48, 56, 64, 88, 96, 104, 120
NL_COLS = 68
NS_COLS = 34
NWSLOT = 3
SAME_ENGINE_WAITS = os.environ.get("KSAME", "1") == "1"


def _freeze(fn):
    if fn.__closure__ is None:
        return fn
    cells = []
    for c in fn.__closure__:
        try:
            cells.append(types.CellType(c.cell_contents))
        except ValueError:
            cells.append(c)
    return types.FunctionType(fn.__code__, fn.__globals__, fn.__name__, fn.__defaults__, tuple(cells))


class Sched:
    def __init__(self):
        self.q = {}
        self.cnt = {}
        self.waited = {}
        self.lastw = {}
        self.readers = {}
        self.sems = set()

    def _eng(self, eng):
        if eng not in self.q:
            self.q[eng] = []
            self.waited[eng] = {}

    def _waits(self, eng, reads, writes, extra):
        self._eng(eng)
        need = {}
        evs = list(extra)
        for r in reads:
            ev = self.lastw.get(r)
            if ev:
                evs.append(ev)
        for w in writes:
            ev = self.lastw.get(w)
            if ev:
                evs.append(ev)
            evs.extend(self.readers.get(w, ()))
        for s, v in evs:
            if need.get(s, 0) < v:
                need[s] = v
        for s, v in need.items():
            if s == "e_" + eng and not SAME_ENGINE_WAITS:
                continue
            if self.waited[eng].get(s, 0) < v:
                self.q[eng].append(("wait", s, v))
                self.waited[eng][s] = v

    def _commit(self, ev, reads, writes):
        for r in reads:
            self.readers.setdefault(r, []).append(ev)
        for w in writes:
            self.lastw[w] = ev
            self.readers[w] = []

    def op(self, eng, fn, reads=(), writes=(), extra=()):
        self._waits(eng, reads, writes, extra)
        sem = "e_" + eng
        self.sems.add(sem)
        self.cnt[sem] = self.cnt.get(sem, 0) + 1
        ev = (sem, self.cnt[sem])
        self.q[eng].append(("op", _freeze(fn), sem, 1))
        self._commit(ev, reads, writes)
        return ev

    def dma(self, eng, fn, sem, reads=(), writes=(), extra=()):
        self._waits(eng, reads, writes, extra)
        self.sems.add(sem)
        self.cnt[sem] = self.cnt.get(sem, 0) + 16
        ev = (sem, self.cnt[sem])
        self.q[eng].append(("op", _freeze(fn), sem, 16))
        self._commit(ev, reads, writes)
        return ev

    def final_wait(self, eng, evs):
        self._waits(eng, (), (), evs)


def build_program():
    nc = bass.Bass("TRN2", target_bir_lowering=False)

    def din(name, shape):
        return nc.dram_tensor(name, list(shape), F32, kind="ExternalInput").ap()

    def dout(name, shape):
        return nc.dram_tensor(name, list(shape), F32, kind="ExternalOutput").ap()

    xp = din("xp", [SEQ, D])
    xs = din("xs", [TS, D])
    pp = din("pp", [DEPTH, SEQ, DPLE])
    ps = din("ps", [DEPTH, TS, DPLE])
    sin = din("sin", [DEPTH, 96, DL])
    win = din("win", [DEPTH, 48, 128, 2048])
    wout = din("wout", [DEPTH, 16, 128, 2048])
    wgt = din("wgt", [DEPTH, 16, 128, 2048])
    wpt = din("wpt", [DEPTH, 16, 128, 256])
    wat = din("wat", [DEPTH, 128, 1024])
    wxt = din("wxt", [DEPTH, 128, 1024])
    cst = din("cst", [DEPTH, 128, NCST])
    lng = din("lng", [DEPTH, D])
    lnb = din("lnb", [DEPTH, D])
    ident_d = din("ident", [128, 128])

    yp = dout("yp", [SEQ, D])
    ys = dout("ys", [TS, D])
    o_l = dout("o_l", [DEPTH, NL_COLS, DL])
    o_s = dout("o_s", [DEPTH, NS_COLS, DL])
    x1p = nc.dram_tensor("x1p", [SEQ, D], F32).ap()
    x1s = nc.dram_tensor("x1s", [TS, D], F32).ap()

    S = Sched()
    KST = int(os.environ.get("KSTAGE", "9"))
    from contextlib import ExitStack
    es = ExitStack()

    def sb(name, shape, dt=F32):
        return es.enter_context(nc.sbuf_tensor(name, list(shape), dt))

    XB = sb("XB", [128, 16, TMAX], BF16)
    R = sb("R", [128, 16, TMAX], F32)
    YR = sb("YR", [128, 16 * TMAX // 2], F32)
    Y = YR[:].bitcast(BF16).rearrange("p (c t) -> p c t", c=16)
    CW4 = TMAX * 2
    VT = [YR[:, 0:2048], YR[:, CW4:CW4 + 2048]]
    LGB = YR[:, 2 * CW4:2 * CW4 + 2048]
    LBB = YR[:, 3 * CW4:3 * CW4 + 2048]
    VTK = [["Y%d" % i for i in range(0, 4)], ["Y%d" % i for i in range(4, 8)]]
    LGK = ["Y%d" % i for i in range(8, 12)]
    LBK = ["Y%d" % i for i in range(12, 16)]
    PT = sb("PT", [128, 2, TMAX], BF16)
    STG = [sb("STG%d" % i, [128, 512]) for i in range(2)]
    PSTG = sb("PSTG", [128, DPLE])
    PSTG2 = sb("PSTG2", [128, DPLE])
    WR = [sb("WR%d" % i, [128, 16, 128], BF16) for i in range(NWSLOT)]
    WPS = [sb("WPS%d" % i, [128, 2, 128], BF16) for i in range(2)]
    WA = sb("WA", [128, 1024], BF16)
    WX = sb("WX", [128, 1024], BF16)
    CST = sb("CST", [128, DEPTH, NCST])
    CL = sb("CL", [128, DEPTH, 8])
    CL2 = sb("CL2", [128, DEPTH, 8])
    IDN = sb("IDN", [128, 128])
    ONESB = sb("ONESB", [128, 128], BF16)
    INST = sb("INST", [128, DEPTH, 8, 96])
    OUTL = sb("OUTL", [128, DEPTH, 8, NL_COLS])
    OUTS = sb("OUTS", [128, DEPTH, 8, NS_COLS])
    XL = sb("XL", [128, 3 + 512])
    XLS = sb("XLS", [128, NSS, 3 + DS])
    CHF = sb("CHF", [128, 2 + TP])
    CHS = sb("CHS", [128, NSS, 2 + DS])
    MISC = CHF[:, 0:1024]
    MISCK = ["CHh", "CHb0", "CHb1"]
    WK = sb("WK", [128, 6, 512])
    XC, RG, IG, SG, T1, HS = [WK[:, i, :] for i in range(6)]
    XCB = sb("XCB", [128, 512], BF16)
    SQ = sb("SQ", [128, 512], BF16)
    FENCE = sb("FENCE", [128, 2])
    SQB = sb("SQB", [128, 512], BF16)
    TT2 = sb("TT2", [128, 512])
    RF = R[:].rearrange("p a b -> p (a b)")
    WK1 = RF[:, 0:3072].rearrange("p (a b) -> p a b", a=6)
    XL1 = RF[:, 3072:3072 + 515]
    XCB1 = RF[:, 3588:3588 + 256].bitcast(BF16)
    SQ1 = RF[:, 3844:3844 + 256].bitcast(BF16)
    VF = RF[:, 4100:4100 + 1536]
    SGF = RF[:, 5636:5636 + 1536]
    SC_TT = [RF[:, 7172:7684], RF[:, 7684:8196], RF[:, 8708:8772]]
    SC_SQ = [RF[:, 8196:8452].bitcast(BF16), RF[:, 8452:8708].bitcast(BF16), RF[:, 8772:8804].bitcast(BF16)]
    WKS = sb("WKS", [128, 6, TS])
    XCBS = sb("XCBS", [128, TS], BF16)
    SQS = sb("SQS", [128, TS], BF16)
    BUFS = {
        0: dict(sfx="", ro=[], XL=XL, XC=XC, RG=RG, IG=IG, SG=SG, T1=T1, HS=HS, XCB=XCB, SQ=SQ),
        1: dict(sfx="1", ro=["RAL"], XL=XL1, XC=WK1[:, 0, :], RG=WK1[:, 1, :], IG=WK1[:, 2, :], SG=WK1[:, 3, :],
                T1=WK1[:, 4, :], HS=WK1[:, 5, :], XCB=XCB1, SQ=SQ1),
        "s": dict(sfx="s", ro=[], XL=XLS, XC=WKS[:, 0, :], RG=WKS[:, 1, :], IG=WKS[:, 2, :], SG=WKS[:, 3, :],
                  T1=WKS[:, 4, :], HS=WKS[:, 5, :], XCB=XCBS, SQ=SQS),
    }
    ST6 = sb("ST6", [128, 4, 6])
    MV = sb("MV", [128, 4])
    PSB = [es.enter_context(nc.psum_tensor("PS%d" % i, [128, 512], F32)) for i in range(8)]

    free_banks = {"p": list(range(6)), "s": [6, 7]}

    def bank(pool="p"):
        assert free_banks[pool], "no released PSUM bank in pool " + pool
        i = free_banks[pool].pop(0)
        return PSB[i], "ps%d" % i

    def rel(bkey):
        i = int(bkey[2:])
        pool = "p" if i < 6 else "s"
        assert i not in free_banks[pool]
        free_banks[pool].append(i)

    wctr = [0]

    def load_w(src):
        i = wctr[0] % NWSLOT
        wctr[0] += 1
        S.dma("pool", lambda e, i=i, src=src: e.dma_start(
            out=WR[i][:].rearrange("p a b -> p (a b)"), in_=src), "w%d" % i,
            writes=["w%d" % i])
        return WR[i], "w%d" % i

    VT3 = VT + [WK[:, 0:4, :].rearrange("p a b -> p (a b)")]
    VTK3 = VTK + [["XC", "RG", "IG", "SG"]]
    stg_ctr = [0]
    pstg_ctr = [0]
    STGS = [(STG[0], "stg0", "dstg0"), (STG[1], "stg1", "dstg1")] + [
        (WK[:, i, :], k, "dstg%d" % (i + 2)) for i, k in enumerate(["XC", "RG", "IG", "SG", "T1", "HS"])]
    STGS0 = [STGS[0], STGS[1], STGS[6], STGS[7],
             (CHF[:, 2:514], "CHb0", "dstg8"), (CHF[:, 514:1026], "CHb1", "dstg9")]
    PSTGS = [PSTG, PSTG2]

    S.dma("sp", lambda e: e.dma_start(out=IDN[:], in_=ident_d), "c0", writes=["IDN"])
    S.dma("sp", lambda e: e.dma_start(out=CST[:], in_=cst.rearrange("l p n -> p l n")), "c1",
          writes=["CST"])
    S.op("dve", lambda e: e.memset(ONESB[:], 1.0 / 128.0), writes=["ONESB"])
    S.op("dve", lambda e: e.memset(OUTL[:], 0.0), writes=["OUTL"])
    S.op("dve", lambda e: e.memset(OUTS[:], 0.0), writes=["OUTS"])
    S.op("act", lambda e: e.activation(out=CL[:], in_=CST[:, :, LAM:LAM + 8], func=AF.Exp, scale=-1.0),
         reads=["CST"], writes=["CL"])
    S.op("act", lambda e: e.activation(out=CL[:], in_=CL[:], func=AF.Ln, bias=1.0, scale=1.0),
         reads=["CL"], writes=["CL"])
    S.op("dve", lambda e: e.tensor_scalar(out=CL2[:], in0=CL[:], scalar1=-16.0, scalar2=None, op0=ALU.mult),
         reads=["CL"], writes=["CL2"])
    S.op("dve", lambda e: e.tensor_scalar(out=CL[:], in0=CL[:], scalar1=-8.0, scalar2=None, op0=ALU.mult),
         reads=["CL", "CL2"], writes=["CL"])
    for l in range(DEPTH):
        S.dma("sp", lambda e, l=l: e.dma_start(out=MISC[0:96, :], in_=sin[l]), "c2",
              writes=MISCK)
        for h in range(NH):
            bk, bkey = bank()
            S.op("pe", lambda e, h=h, bk=bk: e.transpose(
                out=bk[:, 0:96], in_=MISC[0:96, h * 128:(h + 1) * 128], identity=IDN[0:96, 0:96]),
                reads=MISCK + ["IDN"], writes=[bkey])
            S.op("act", lambda e, l=l, h=h, bk=bk: e.copy(out=INST[:, l, h, :], in_=bk[:, 0:96]),
                 reads=[bkey], writes=["INST"])
            rel(bkey)

    out_events = []

    def tile_rts(tl):
        return [(i * 128, 128, False) for i in range(8)] + ([(1024, TS, True)] if tl == 0 else [])

    def load_x_T(l, tl, c0, nr, is_s, to_R, slots, qs=(0, 1, 2, 3)):
        srcp, srcs = (xp, xs) if l == 0 else (x1p, x1s)
        r0 = tl * TP + c0
        for q in qs:
            si = stg_ctr[0] % len(slots)
            stg_ctr[0] += 1
            stg_t, stg_k, stg_sem = slots[si]
            src = srcs[0:nr, q * 512:(q + 1) * 512] if is_s else srcp[r0:r0 + nr, q * 512:(q + 1) * 512]
            rk = ["x1s" if is_s else "x1p%d" % (r0 // 128)] if l == 1 else []
            S.dma("sp", lambda e, stg_t=stg_t, src=src, nr=nr: e.dma_start(out=stg_t[0:nr, :], in_=src),
                  stg_sem, reads=rk, writes=[stg_k])
            bk, bkey = bank()

            def tr(e, stg_t=stg_t, bk=bk, nr=nr):
                ins = None
                for j in range(4):
                    ins = e.transpose(out=bk[:, j * 128:j * 128 + nr],
                                      in_=stg_t[0:nr, j * 128:(j + 1) * 128],
                                      identity=IDN[0:nr, 0:nr])
                return ins
            S.op("pe", tr, reads=[stg_k, "IDN"], writes=[bkey])
            pv = bk[:].rearrange("p (j t) -> p j t", j=4)[:, :, 0:nr]
            if to_R:
                S.op("act", lambda e, q=q, c0=c0, nr=nr, pv=pv: e.copy(
                    out=R[:, 4 * q:4 * q + 4, c0:c0 + nr], in_=pv),
                    reads=[bkey], writes=["R%d" % d for d in range(4 * q, 4 * q + 4)] + ["RAL"])
            else:
                S.op("act", lambda e, q=q, c0=c0, nr=nr, pv=pv: e.copy(
                    out=XB[:, 4 * q:4 * q + 4, c0:c0 + nr], in_=pv),
                    reads=[bkey], writes=["XB%d" % d for d in range(4 * q, 4 * q + 4)])
            rel(bkey)

    def phase0_rt(l, tl, ri):
        c0, nr, is_s = tile_rts(tl)[ri]
        r0 = tl * TP + c0
        load_x_T(l, tl, c0, nr, is_s, False, STGS0)
        psrc = ps[l, 0:nr, :] if is_s else pp[l, r0:r0 + nr, :]
        pi = pstg_ctr[0] % 2
        pstg_ctr[0] += 1
        pst = PSTGS[pi]
        S.dma("sp", lambda e, pst=pst, psrc=psrc, nr=nr: e.dma_start(out=pst[0:nr, :], in_=psrc), "pstg%d" % pi,
              writes=["pstg%d" % pi])
        bk, bkey = bank()

        def trp(e, pst=pst, bk=bk, nr=nr):
            ins = None
            for j in range(2):
                ins = e.transpose(out=bk[:, j * 128:j * 128 + nr], in_=pst[0:nr, j * 128:(j + 1) * 128],
                                  identity=IDN[0:nr, 0:nr])
            return ins
        S.op("pe", trp, reads=["pstg%d" % pi, "IDN"], writes=[bkey])
        pv = bk[:, 0:256].rearrange("p (j t) -> p j t", j=2)[:, :, 0:nr]
        S.op("dve", lambda e, c0=c0, nr=nr, pv=pv: e.tensor_copy(out=PT[:, :, c0:c0 + nr], in_=pv),
             reads=[bkey], writes=["PT"])
        rel(bkey)

    def run_tile(l, tl, first_tile, nxt_tile):
        has_s = (tl == 0)
        T = TP + (TS if has_s else 0)
        segs = [(0, 512, False), (512, 512, False)] + ([(1024, TS, True)] if has_s else [])
        rts = [(i * 128, 128, False) for i in range(8)] + ([(1024, TS, True)] if has_s else [])
        srcp, srcs = (xp, xs) if l == 0 else (x1p, x1s)
        dstp, dsts = (x1p, x1s) if l == 0 else (yp, ys)
        C = lambda off, n=1: CST[:, l, off:off + n]

        if True:
            S.dma("pool", lambda e: e.dma_start(out=WA[:], in_=wat[l]), "wa", writes=["WA"])
            S.dma("pool", lambda e: e.dma_start(out=WX[:], in_=wxt[l]), "wx", writes=["WX"])

        if first_tile:
            for ri0 in range(len(rts)):
                phase0_rt(l, tl, ri0)

        XBK = ["XB%d" % d for d in range(16)]

        def zmm(wt, wkey, rhs_of_kc, n, rkeys, nk=16, pool="p"):
            bk, bkey = bank(pool)

            def f(e, bk=bk):
                ins = None
                for kc in range(nk):
                    ins = e.matmul(bk[:, 0:n], lhsT=wt[:, kc, :], rhs=rhs_of_kc(kc),
                                   start=(kc == 0), stop=(kc == nk - 1))
                return ins
            S.op("pe", f, reads=[wkey] + rkeys, writes=[bkey])
            return bk, bkey

        def group_norm_store(ysrc, ykeys, n, gcol, ych, c0, TT=None, tkeys=("T1",)):
            TT = T1 if TT is None else TT
            tkeys = list(tkeys)
            ykeys = list(ykeys)
            S.op("act", lambda e: e.activation(out=SQ[:, 0:n], in_=ysrc, func=AF.Square),
                 reads=ykeys, writes=["SQ"])
            bk, bkey = bank()
            S.op("pe", lambda e, bk=bk: e.matmul(bk[:, 0:n], lhsT=ONESB[:], rhs=SQ[:, 0:n], start=True, stop=True),
                 reads=["SQ", "ONESB"], writes=[bkey])
            S.op("act", lambda e, bk=bk: e.activation(out=TT[:, 0:n], in_=bk[:, 0:n], func=AF.Ln,
                                                      bias=GN_EPS, scale=1.0),
                 reads=[bkey], writes=tkeys)
            rel(bkey)
            S.op("act", lambda e: e.activation(out=TT[:, 0:n], in_=TT[:, 0:n], func=AF.Exp, scale=-0.5),
                 reads=tkeys, writes=tkeys)
            S.op("dve", lambda e: e.scalar_tensor_tensor(
                out=Y[:, ych, c0:c0 + n], in0=ysrc, scalar=C(gcol), in1=TT[:, 0:n],
                op0=ALU.mult, op1=ALU.mult),
                reads=ykeys + tkeys + ["CST"], writes=["Y%d" % ych])

        S.op("dve", lambda e: e.memset(FENCE[:], 0.0), writes=["R%d" % d for d in range(16)] + ["RAL"])
        STR = []
        for si_, (c0, n, is_s) in enumerate(segs):
            B = dict(BUFS["s" if is_s else si_])
            B.update(c0=c0, n=n, is_s=is_s)
            STR.append(B)

        def kx(B, *names):
            return [nm + B["sfx"] for nm in names] + B["ro"]

        zb = {}

        def emit_Zx(h):
            wx_t, wx_k = load_w(win[l, h])
            for B in STR:
                rf = lambda kc, B=B: XB[:, kc, B["c0"]:B["c0"] + B["n"]]
                zb[(h, B["sfx"], "x")] = zmm(wx_t, wx_k, rf, B["n"], XBK, pool="s" if B["is_s"] else "p")

        def emit_Zg(h):
            wg_t, wg_k = load_w(win[l, 8 + h])
            for B in STR:
                rf = lambda kc, B=B: XB[:, kc, B["c0"]:B["c0"] + B["n"]]
                zb[(h, B["sfx"], "g")] = zmm(wg_t, wg_k, rf, B["n"], XBK, pool="s" if B["is_s"] else "p")

        def fx_halo(h):
            OL = OUTL[:, l, h, :]
            for B in STR:
                if B["is_s"]:
                    S.op("dve", lambda e, B=B: e.tensor_copy(
                        out=B["XL"][:, :, 0:3], in_=INST[:, l, h, 0:48].rearrange("p (s k) -> p s k", k=3)),
                        reads=["INST"] + B["ro"], writes=kx(B, "XLh"))
                elif B["c0"] == 0:
                    S.op("dve", lambda e, B=B, OL=OL: e.tensor_copy(out=B["XL"][:, 0:3], in_=OL[:, 1:4]),
                         reads=["OUTL"] + B["ro"], writes=kx(B, "XLh"))

        def fx_evac(h, B):
            bx, bxk = zb[(h, B["sfx"], "x")]
            n = B["n"]
            if B["is_s"]:
                bx3 = bx[:, 0:TS].rearrange("p (s t) -> p s t", t=DS)
                S.op("act", lambda e, B=B, bx3=bx3: e.copy(out=B["XL"][:, :, 3:3 + DS], in_=bx3),
                     reads=[bxk] + B["ro"], writes=kx(B, "XLb"))
            else:
                S.op("act", lambda e, B=B, bx=bx, n=n: e.copy(out=B["XL"][:, 3:3 + n], in_=bx[:, 0:n]),
                     reads=[bxk] + B["ro"], writes=kx(B, "XLb"))
            rel(bxk)

        def fx_conv(h, B):
            n = B["n"]
            if (not B["is_s"]) and B["c0"] > 0:
                B0 = STR[0]
                S.op("dve", lambda e, B0=B0, B=B: e.tensor_copy(out=B["XL"][:, 0:3],
                                                                in_=B0["XL"][:, B0["n"]:B0["n"] + 3]),
                     reads=kx(B0, "XLb") + B["ro"], writes=kx(B, "XLh"))
            if B["is_s"]:
                taps = [B["XL"][:, :, k:k + DS] for k in range(4)]
                xc = B["XC"][:, 0:n].rearrange("p (s t) -> p s t", t=DS)
            else:
                taps = [B["XL"][:, k:k + n] for k in range(4)]
                xc = B["XC"][:, 0:n]
            S.op("dve", lambda e, taps=taps, xc=xc: e.tensor_scalar(
                out=xc, in0=taps[3], scalar1=C(CW + h * 4 + 3), scalar2=C(CB + h),
                op0=ALU.mult, op1=ALU.add), reads=kx(B, "XLb", "XLh") + ["CST"], writes=kx(B, "XC"))
            for k in (2, 1, 0):
                S.op("dve", lambda e, taps=taps, xc=xc, k=k: e.scalar_tensor_tensor(
                    out=xc, in0=taps[k], scalar=C(CW + h * 4 + k), in1=xc, op0=ALU.mult, op1=ALU.add),
                    reads=kx(B, "XLb", "XLh", "XC") + ["CST"], writes=kx(B, "XC"))

        def fx_cast(h, B):
            n = B["n"]
            S.op("act", lambda e, B=B, n=n: e.copy(out=B["XCB"][:, 0:n], in_=B["XC"][:, 0:n]),
                 reads=kx(B, "XC"), writes=kx(B, "XCB"))

        def fx_finish(h):
            OL = OUTL[:, l, h, :]
            for B in STR:
                if B["is_s"]:
                    S.op("dve", lambda e, B=B, OL=OL: e.tensor_copy(
                        out=OL[:, 20:68].rearrange("p (s k) -> p s k", k=3), in_=B["XL"][:, :, 4:7]),
                        reads=kx(B, "XLb"), writes=["OUTL"])
            Bl = [B for B in STR if not B["is_s"]][-1]
            S.op("dve", lambda e, Bl=Bl, OL=OL: e.tensor_copy(out=OL[:, 1:4], in_=Bl["XL"][:, Bl["n"]:Bl["n"] + 3]),
                 reads=kx(Bl, "XLb"), writes=["OUTL"])

        def front_x(h):
            fx_halo(h)
            for B in STR:
                fx_evac(h, B)
            for B in STR:
                fx_conv(h, B)
                fx_cast(h, B)
            fx_finish(h)

        def gates(h):
            for B in STR:
                n = B["n"]
                if B["is_s"]:
                    for nm, W_, col, dst in (("r", WA, BA, "RG"), ("i", WX, BX, "IG")):
                        bk, bkey = bank("s")
                        S.op("pe", lambda e, B=B, bk=bk, n=n, W_=W_: e.matmul(
                            bk[:, 0:n], lhsT=W_[:, h * 128:(h + 1) * 128], rhs=B["XCB"][:, 0:n], start=True, stop=True),
                            reads=["WA", "WX"] + kx(B, "XCB"), writes=[bkey])
                        S.op("act", lambda e, B=B, bk=bk, n=n, col=col, dst=dst: e.activation(
                            out=B[dst][:, 0:n], in_=bk[:, 0:n], func=AF.Sigmoid, bias=C(col + h), scale=1.0),
                            reads=[bkey, "CST"] + B["ro"], writes=kx(B, dst))
                        rel(bkey)
                    continue
                br, brk = bank("p")
                S.op("pe", lambda e, B=B, br=br, n=n: e.matmul(br[:, 0:n], lhsT=WA[:, h * 128:(h + 1) * 128],
                                                               rhs=B["XCB"][:, 0:n], start=True, stop=True),
                     reads=["WA"] + kx(B, "XCB"), writes=[brk])
                bi, bik = bank("p")
                S.op("pe", lambda e, B=B, bi=bi, n=n: e.matmul(bi[:, 0:n], lhsT=WX[:, h * 128:(h + 1) * 128],
                                                               rhs=B["XCB"][:, 0:n], start=True, stop=True),
                     reads=["WX"] + kx(B, "XCB"), writes=[bik])
                zb[(h, B["sfx"], "r")] = (br, brk)
                zb[(h, B["sfx"], "i")] = (bi, bik)

        def sig_ri(h, B):
            br, brk = zb[(h, B["sfx"], "r")]
            bi, bik = zb[(h, B["sfx"], "i")]
            n = B["n"]
            S.op("act", lambda e, B=B, br=br, n=n: e.activation(out=B["RG"][:, 0:n], in_=br[:, 0:n],
                                                                func=AF.Sigmoid, bias=C(BA + h), scale=1.0),
                 reads=[brk, "CST"] + B["ro"], writes=kx(B, "RG"))
            S.op("act", lambda e, B=B, bi=bi, n=n: e.activation(out=B["IG"][:, 0:n], in_=bi[:, 0:n],
                                                                func=AF.Sigmoid, bias=C(BX + h), scale=1.0),
                 reads=[bik, "CST"] + B["ro"], writes=kx(B, "IG"))
            rel(brk)
            rel(bik)

        def front_g(h):
            for B in STR:
                bg, bgk = zb[(h, B["sfx"], "g")]
                n = B["n"]
                S.op("act", lambda e, B=B, bg=bg, n=n: e.activation(out=B["SG"][:, 0:n], in_=bg[:, 0:n],
                                                                    func=AF.Sigmoid),
                     reads=[bgk] + B["ro"], writes=kx(B, "SG"))
            for B in STR:
                bg, bgk = zb[(h, B["sfx"], "g")]
                n = B["n"]
                S.op("dve", lambda e, B=B, bg=bg, n=n: e.tensor_tensor(out=B["SG"][:, 0:n], in0=B["SG"][:, 0:n],
                                                                       in1=bg[:, 0:n], op=ALU.mult),
                     reads=[bgk] + kx(B, "SG"), writes=kx(B, "SG"))
                rel(bgk)

        def mid_a(h):
            for B in STR:
                if not B["is_s"]:
                    sig_ri(h, B)
            for B in STR:
                n = B["n"]
                S.op("dve", lambda e, B=B, n=n: e.tensor_tensor(out=B["IG"][:, 0:n], in0=B["IG"][:, 0:n],
                                                                in1=B["XC"][:, 0:n], op=ALU.mult),
                     reads=kx(B, "IG", "XC"), writes=kx(B, "IG"))

        def mid_chain(h, B):
            n = B["n"]
            S.op("act", lambda e, B=B, n=n: e.activation(out=B["RG"][:, 0:n], in_=B["RG"][:, 0:n], func=AF.Exp,
                                                         scale=CL[:, l, h:h + 1]),
                 reads=kx(B, "RG") + ["CL"], writes=kx(B, "RG"))
            S.op("act", lambda e, B=B, n=n: e.activation(out=B["T1"][:, 0:n], in_=B["RG"][:, 0:n], func=AF.Square),
                 reads=kx(B, "RG"), writes=kx(B, "T1"))
            S.op("act", lambda e, B=B, n=n: e.activation(out=B["T1"][:, 0:n], in_=B["T1"][:, 0:n], func=AF.Ln,
                                                         bias=1.0, scale=-1.0),
                 reads=kx(B, "T1"), writes=kx(B, "T1"))
            S.op("act", lambda e, B=B, n=n: e.activation(out=B["T1"][:, 0:n], in_=B["T1"][:, 0:n], func=AF.Exp,
                                                         scale=0.5),
                 reads=kx(B, "T1"), writes=kx(B, "T1"))

        def mid_tail(h, B):
            OL = OUTL[:, l, h, :]
            n = B["n"]
            S.op("dve", lambda e, B=B, n=n: e.tensor_tensor(out=B["IG"][:, 0:n], in0=B["IG"][:, 0:n],
                                                            in1=B["T1"][:, 0:n], op=ALU.mult),
                 reads=kx(B, "IG", "T1"), writes=kx(B, "IG"))
            if not B["is_s"]:
                S.op("dve", lambda e, B=B, n=n, OL=OL: e.tensor_tensor_scan(
                    out=B["HS"][:, 0:n], data0=B["RG"][:, 0:n], data1=B["IG"][:, 0:n], initial=OL[:, 0:1],
                    op0=ALU.mult, op1=ALU.add), reads=kx(B, "RG", "IG") + ["OUTL"], writes=kx(B, "HS"))
                S.op("dve", lambda e, B=B, n=n, OL=OL: e.tensor_copy(out=OL[:, 0:1], in_=B["HS"][:, n - 1:n]),
                     reads=kx(B, "HS"), writes=["OUTL"])
            else:
                a3 = B["RG"][:, 0:n].rearrange("p (s t) -> p s t", t=DS)
                u3 = B["IG"][:, 0:n].rearrange("p (s t) -> p s t", t=DS)
                h3 = B["HS"][:, 0:n].rearrange("p (s t) -> p s t", t=DS)
                S.op("dve", lambda e, a3=a3, h3=h3: e.tensor_tensor(
                    out=h3[:, :, 0], in0=a3[:, :, 0], in1=INST[:, l, h, 48:64], op=ALU.mult),
                    reads=kx(B, "RG", "HS") + ["INST"], writes=kx(B, "HS"))
                S.op("dve", lambda e, u3=u3, h3=h3: e.tensor_tensor(
                    out=u3[:, :, 0], in0=u3[:, :, 0], in1=h3[:, :, 0], op=ALU.add),
                    reads=kx(B, "IG", "HS"), writes=kx(B, "IG"))
                S.op("dve", lambda e, a3=a3: e.memset(a3[:, :, 0], 0.0),
                     reads=kx(B, "RG", "HS"), writes=kx(B, "RG"))
                S.op("dve", lambda e, B=B, n=n: e.tensor_tensor_scan(
                    out=B["HS"][:, 0:n], data0=B["RG"][:, 0:n], data1=B["IG"][:, 0:n], initial=0.0,
                    op0=ALU.mult, op1=ALU.add), reads=kx(B, "RG", "IG", "HS"), writes=kx(B, "HS"))
                S.op("dve", lambda e, OL=OL, h3=h3: e.tensor_copy(out=OL[:, 4:20], in_=h3[:, :, DS - 1]),
                     reads=kx(B, "HS"), writes=["OUTL"])
            S.op("dve", lambda e, B=B, n=n: e.tensor_tensor(out=B["HS"][:, 0:n], in0=B["HS"][:, 0:n],
                                                            in1=B["SG"][:, 0:n], op=ALU.mult),
                 reads=kx(B, "HS", "SG"), writes=kx(B, "HS"))

        def mid_sq(h, B):
            n = B["n"]
            S.op("act", lambda e, B=B, n=n: e.activation(out=B["SQ"][:, 0:n], in_=B["HS"][:, 0:n], func=AF.Square),
                 reads=kx(B, "HS"), writes=kx(B, "SQ"))

        def gn_mm(h):
            for B in STR:
                n = B["n"]
                bk, bkey = bank("s" if B["is_s"] else "p")
                S.op("pe", lambda e, B=B, bk=bk, n=n: e.matmul(bk[:, 0:n], lhsT=ONESB[:], rhs=B["SQ"][:, 0:n],
                                                               start=True, stop=True),
                     reads=kx(B, "SQ") + ["ONESB"], writes=[bkey])
                S.op("act", lambda e, B=B, bk=bk, n=n: e.copy(out=B["T1"][:, 0:n], in_=bk[:, 0:n]),
                     reads=[bkey] + B["ro"], writes=kx(B, "T1"))
                rel(bkey)

        def back(h):
            for B in STR:
                n = B["n"]
                S.op("act", lambda e, B=B, n=n: e.activation(out=B["T1"][:, 0:n], in_=B["T1"][:, 0:n], func=AF.Ln,
                                                             bias=GN_EPS, scale=1.0),
                     reads=kx(B, "T1"), writes=kx(B, "T1"))
            for B in STR:
                n = B["n"]
                S.op("act", lambda e, B=B, n=n: e.activation(out=B["T1"][:, 0:n], in_=B["T1"][:, 0:n], func=AF.Exp,
                                                             scale=-0.5),
                     reads=kx(B, "T1"), writes=kx(B, "T1"))
            for B in STR:
                n = B["n"]
                S.op("dve", lambda e, B=B, n=n: e.scalar_tensor_tensor(
                    out=Y[:, h, B["c0"]:B["c0"] + n], in0=B["HS"][:, 0:n], scalar=C(GNL + h), in1=B["T1"][:, 0:n],
                    op0=ALU.mult, op1=ALU.mult),
                    reads=kx(B, "HS", "T1") + ["CST"], writes=["Y%d" % h])

        YK = ["Y%d" % d for d in range(16)]
        RAL = ["RAL"]
        vk = lambda c0: ["VF%d" % (c0 // 512)] + RAL
        sk = lambda c0: ["SGF%d" % (c0 // 512)] + RAL
        VKA = ["VF0", "VF1", "VF2"] + RAL
        SQL = [(SC_SQ[i], "SCQ%d" % i) for i in range(3)]
        TTL = [(SC_TT[i], ["SCT%d" % i]) for i in range(3)]

        def sc_gn(g):
            for si_, (c0, n, is_s) in enumerate(segs):
                sq, sqk = SQL[si_]
                bk, bkey = bank("s" if is_s else "p")
                S.op("pe", lambda e, bk=bk, sq=sq, n=n: e.matmul(bk[:, 0:n], lhsT=ONESB[:], rhs=sq[:, 0:n],
                                                                 start=True, stop=True),
                     reads=[sqk, "ONESB"] + RAL, writes=[bkey])
                tt, tk = TTL[si_]
                S.op("act", lambda e, bk=bk, tt=tt, n=n: e.copy(out=tt[:, 0:n], in_=bk[:, 0:n]),
                     reads=[bkey] + RAL, writes=tk)
                rel(bkey)

        def sc_fin(g):
            for si_, (c0, n, is_s) in enumerate(segs):
                tt, tk = TTL[si_]
                S.op("act", lambda e, tt=tt, n=n: e.activation(out=tt[:, 0:n], in_=tt[:, 0:n], func=AF.Ln,
                                                               bias=GN_EPS, scale=1.0),
                     reads=tk + RAL, writes=tk)
            for si_, (c0, n, is_s) in enumerate(segs):
                tt, tk = TTL[si_]
                S.op("act", lambda e, tt=tt, n=n: e.activation(out=tt[:, 0:n], in_=tt[:, 0:n], func=AF.Exp, scale=-0.5),
                     reads=tk + RAL, writes=tk)
            for si_, (c0, n, is_s) in enumerate(segs):
                tt, tk = TTL[si_]
                S.op("dve", lambda e, tt=tt, c0=c0, n=n, g=g: e.scalar_tensor_tensor(
                    out=Y[:, 8 + g, c0:c0 + n], in0=VF[:, c0:c0 + n], scalar=C(GNS + g), in1=tt[:, 0:n],
                    op0=ALU.mult, op1=ALU.mult),
                    reads=vk(c0) + tk + ["CST"], writes=["Y%d" % (8 + g)])

        def sc_part(g, part):
            OSg = OUTS[:, l, g, :]
            if part == 0:
                S.op("dve", lambda e, OSg=OSg: e.tensor_copy(out=CHF[:, 0:2], in_=OSg[:, 0:2]),
                     reads=["OUTS"], writes=["CHh"])
                if has_s:
                    S.op("dve", lambda e, g=g: e.tensor_copy(
                        out=CHS[:, :, 0:2], in_=INST[:, l, g, 64:96].rearrange("p (s k) -> p s k", k=2)),
                        reads=["INST"], writes=["CHSh"])
                wt, wk = load_w(win[l, 24 + g])
                for (c0, n, is_s) in segs:
                    bk, bkey = zmm(wt, wk, lambda kc, c0=c0, n=n: XB[:, kc, c0:c0 + n], n, XBK,
                                   pool="s" if is_s else "p")
                    if not is_s:
                        S.op("act", lambda e, bk=bk, c0=c0, n=n: e.copy(out=CHF[:, 2 + c0:2 + c0 + n], in_=bk[:, 0:n]),
                             reads=[bkey], writes=["CHb%d" % (c0 // 512)])
                    else:
                        S.op("act", lambda e, bk=bk: e.copy(
                            out=CHS[:, :, 2:2 + DS], in_=bk[:, 0:TS].rearrange("p (s t) -> p s t", t=DS)),
                            reads=[bkey], writes=["CHSb"])
                    rel(bkey)
            elif part == 1:
                wt, wk = load_w(win[l, 32 + g])
                for (c0, n, is_s) in segs:
                    bk, bkey = zmm(wt, wk, lambda kc, c0=c0, n=n: XB[:, kc, c0:c0 + n], n, XBK,
                                   pool="s" if is_s else "p")
                    if not is_s:
                        S.op("dve", lambda e, bk=bk, c0=c0, n=n: e.tensor_tensor(
                            out=CHF[:, 2 + c0:2 + c0 + n], in0=CHF[:, 2 + c0:2 + c0 + n], in1=bk[:, 0:n], op=ALU.mult),
                            reads=[bkey, "CHb%d" % (c0 // 512)], writes=["CHb%d" % (c0 // 512)])
                    else:
                        S.op("dve", lambda e, bk=bk: e.tensor_tensor(
                            out=CHS[:, :, 2:2 + DS], in0=CHS[:, :, 2:2 + DS],
                            in1=bk[:, 0:TS].rearrange("p (s t) -> p s t", t=DS), op=ALU.mult),
                            reads=[bkey, "CHSb"], writes=["CHSb"])
                    rel(bkey)

                def conv3(taps, out, rk, wkeys):
                    S.op("dve", lambda e: e.tensor_scalar(out=out, in0=taps[2], scalar1=C(SCW + g * 3 + 2),
                                                          scalar2=None, op0=ALU.mult),
                         reads=rk + ["CST"] + RAL, writes=wkeys)
                    for k in (1, 0):
                        S.op("dve", lambda e, k=k: e.scalar_tensor_tensor(
                            out=out, in0=taps[k], scalar=C(SCW + g * 3 + k), in1=out, op0=ALU.mult, op1=ALU.add),
                            reads=rk + ["CST"] + RAL + wkeys, writes=wkeys)
                conv3([CHF[:, k:k + TP] for k in range(3)], VF[:, 0:TP], ["CHh", "CHb0", "CHb1"], ["VF0", "VF1"])
                S.op("dve", lambda e, OSg=OSg: e.tensor_copy(out=OSg[:, 0:2], in_=CHF[:, TP:TP + 2]),
                     reads=["CHb1"], writes=["OUTS"])
                if has_s:
                    conv3([CHS[:, :, k:k + DS] for k in range(3)],
                          VF[:, TP:TP + TS].rearrange("p (s t) -> p s t", t=DS), ["CHSh", "CHSb"], ["VF2"])
                    S.op("dve", lambda e, OSg=OSg: e.tensor_copy(
                        out=OSg[:, 2:34].rearrange("p (s k) -> p s k", k=2), in_=CHS[:, :, 4:6]),
                        reads=["CHSb"], writes=["OUTS"])
            elif part == 2:
                wt, wk = load_w(win[l, 40 + g])
                for (c0, n, is_s) in segs:
                    bk, bkey = zmm(wt, wk, lambda kc, c0=c0, n=n: XB[:, kc, c0:c0 + n], n, XBK,
                                   pool="s" if is_s else "p")
                    S.op("act", lambda e, bk=bk, c0=c0, n=n: e.activation(out=SGF[:, c0:c0 + n], in_=bk[:, 0:n],
                                                                          func=AF.Sigmoid),
                         reads=[bkey] + RAL, writes=sk(c0)[:1])
                    S.op("dve", lambda e, bk=bk, c0=c0, n=n: e.tensor_tensor(
                        out=SGF[:, c0:c0 + n], in0=SGF[:, c0:c0 + n], in1=bk[:, 0:n], op=ALU.mult),
                        reads=[bkey] + sk(c0), writes=sk(c0)[:1])
                    rel(bkey)
            else:
                wt, wk = load_w(win[l, 16 + g])
                for si_, (c0, n, is_s) in enumerate(segs):
                    bk, bkey = zmm(wt, wk, lambda kc, c0=c0, n=n: XB[:, kc, c0:c0 + n], n, XBK,
                                   pool="s" if is_s else "p")
                    S.op("dve", lambda e, bk=bk, c0=c0, n=n: e.tensor_tensor(
                        out=VF[:, c0:c0 + n], in0=VF[:, c0:c0 + n], in1=bk[:, 0:n], op=ALU.mult),
                        reads=[bkey] + vk(c0), writes=vk(c0)[:1])
                    rel(bkey)
                    S.op("dve", lambda e, c0=c0, n=n: e.tensor_tensor(
                        out=VF[:, c0:c0 + n], in0=VF[:, c0:c0 + n], in1=SGF[:, c0:c0 + n], op=ALU.mult),
                        reads=vk(c0) + sk(c0), writes=vk(c0)[:1])
                    sq, sqk = SQL[si_]
                    S.op("act", lambda e, sq=sq, c0=c0, n=n: e.activation(out=sq[:, 0:n], in_=VF[:, c0:c0 + n],
                                                                          func=AF.Square),
                         reads=vk(c0), writes=[sqk])
                pass

        def lru_sc_pipeline():
            emit_Zx(0)
            emit_Zg(0)
            front_x(0)
            front_g(0)
            gates(0)
            if NH > 1:
                emit_Zx(1)
            for h in range(NH):
                nxt = h + 1 < NH
                mid_a(h)
                if nxt:
                    emit_Zg(h + 1)
                if h >= 1:
                    back(h - 1)
                for B in STR:
                    mid_chain(h, B)
                if h >= 1:
                    sc_fin(h - 1)
                if nxt:
                    fx_halo(h + 1)
                    for B in STR:
                        fx_evac(h + 1, B)
                sc_next = 0
                for B in STR:
                    mid_tail(h, B)
                    mid_sq(h, B)
                    if nxt:
                        fx_conv(h + 1, B)
                        fx_cast(h + 1, B)
                    if sc_next < 2:
                        sc_part(h, sc_next)
                        sc_next += 1
                if nxt:
                    fx_finish(h + 1)
                while sc_next < 2:
                    sc_part(h, sc_next)
                    sc_next += 1
                if nxt:
                    front_g(h + 1)
                sc_part(h, 2)
                sc_part(h, 3)
                gn_mm(h)
                if nxt:
                    gates(h + 1)
                sc_gn(h)
                if h + 2 < NH:
                    emit_Zx(h + 2)
            back(NH - 1)
            sc_fin(NH - 1)

        if KST >= 4:
            lru_sc_pipeline()

        for q4 in range(4):
            for (c0, nr, is_s) in (rts if KST >= 2 else []):
                load_x_T(l, tl, c0, nr, is_s, True, STGS, qs=(q4,))
            if KST < 5:
                continue
            for d in range(4 * q4, 4 * q4 + 4):
                wt, wk = load_w(wout[l, d])
                for (c0, n, is_s) in segs:
                    bk, bkey = zmm(wt, wk, lambda kc, c0=c0, n=n: Y[:, kc, c0:c0 + n], n, YK)
                    S.op("dve", lambda e, bk=bk, d=d, c0=c0, n=n: e.scalar_tensor_tensor(
                        out=R[:, d, c0:c0 + n], in0=R[:, d, c0:c0 + n], scalar=float(ALPHA), in1=bk[:, 0:n],
                        op0=ALU.mult, op1=ALU.add), reads=[bkey, "R%d" % d], writes=["R%d" % d])
                    rel(bkey)
                    S.op("act", lambda e, d=d, c0=c0, n=n: e.copy(out=XB[:, d, c0:c0 + n], in_=R[:, d, c0:c0 + n]),
                         reads=["R%d" % d], writes=["XB%d" % d])


        for d in range(16 if KST >= 6 else 0):
            wt, wk = load_w(wgt[l, d])
            wi = d % 2
            S.dma("pool", lambda e, wi=wi, d=d: e.dma_start(
                out=WPS[wi][:].rearrange("p a b -> p (a b)"), in_=wpt[l, d]), "wp%d" % wi, writes=["wp%d" % wi])
            for (c0, n, is_s) in segs:
                bk, bkey = zmm(wt, wk, lambda kc, c0=c0, n=n: XB[:, kc, c0:c0 + n], n, XBK)
                be, bekey = zmm(WPS[wi], "wp%d" % wi, lambda kc, c0=c0, n=n: PT[:, kc, c0:c0 + n], n, ["PT"], nk=2)
                S.op("act", lambda e, bk=bk, d=d, n=n: e.activation(out=T1[:, 0:n], in_=bk[:, 0:n], func=AF.Sigmoid,
                                                                    bias=C(BG + d), scale=1.0),
                     reads=[bkey, "CST"], writes=["T1"])
                S.op("dve", lambda e, be=be, n=n: e.tensor_tensor(out=T1[:, 0:n], in0=T1[:, 0:n], in1=be[:, 0:n],
                                                                  op=ALU.mult), reads=[bekey, "T1"], writes=["T1"])
                rel(bkey)
                rel(bekey)
                S.op("dve", lambda e, d=d, c0=c0, n=n: e.tensor_tensor(
                    out=R[:, d, c0:c0 + n], in0=R[:, d, c0:c0 + n], in1=T1[:, 0:n], op=ALU.add),
                    reads=["T1", "R%d" % d], writes=["R%d" % d])

        RK = ["R%d" % d for d in range(16)]
        S.dma("sp", lambda e: e.dma_start(out=LGB, in_=lng[l:l + 1, :].partition_broadcast(128)), "c3",
              reads=[], writes=LGK)
        S.dma("sp", lambda e: e.dma_start(out=LBB, in_=lnb[l:l + 1, :].partition_broadcast(128)), "c3b",
              reads=[], writes=LBK)
        rts4 = rts if (KST >= 2 and os.environ.get('KP4', '1') == '1') else []

        def p4_setup(ri):
            c0, nr, is_s = rts4[ri]
            return c0, nr, is_s, tl * TP + c0, VT3[ri % 3], VTK3[ri % 3]

        def p4_stage_a(ri):
            c0, nr, is_s, r0, vt, vk = p4_setup(ri)
            for q in range(4):
                bk, bkey = bank()

                def tr(e, bk=bk, q=q, c0=c0, nr=nr):
                    ins = None
                    for j in range(4):
                        ins = e.transpose(out=bk[0:nr, j * 128:(j + 1) * 128], in_=R[:, 4 * q + j, c0:c0 + nr],
                                          identity=IDN[:, :])
                    return ins
                S.op("pe", tr, reads=RK[4 * q:4 * q + 4] + ["IDN"], writes=[bkey])
                S.op("act", lambda e, bk=bk, q=q, nr=nr, vt=vt: e.copy(out=vt[0:nr, q * 512:(q + 1) * 512],
                                                                       in_=bk[0:nr, :]),
                     reads=[bkey], writes=vk)
                rel(bkey)

        def p4_stage_b(ri):
            c0, nr, is_s, r0, vt, vk = p4_setup(ri)
            for q in range(4):
                S.op("dve", lambda e, q=q, nr=nr, vt=vt: e.bn_stats(out=ST6[0:nr, q, :],
                                                                    in_=vt[0:nr, q * 512:(q + 1) * 512]),
                     reads=vk, writes=["ST6"])
            S.op("dve", lambda e, nr=nr: e.bn_aggr(out=MV[0:nr, 0:2], in_=ST6[0:nr, :, :]),
                 reads=["ST6"], writes=["MV"])
            S.op("act", lambda e, nr=nr: e.activation(out=MV[0:nr, 2:3], in_=MV[0:nr, 1:2], func=AF.Ln,
                                                      bias=LN_EPS, scale=1.0), reads=["MV"], writes=["MV2"])
            S.op("act", lambda e, nr=nr: e.activation(out=MV[0:nr, 2:3], in_=MV[0:nr, 2:3], func=AF.Exp,
                                                      scale=-0.5), reads=["MV2"], writes=["MV2"])
            S.op("dve", lambda e, nr=nr: e.scalar_tensor_tensor(
                out=MV[0:nr, 3:4], in0=MV[0:nr, 0:1], scalar=-1.0, in1=MV[0:nr, 2:3], op0=ALU.mult, op1=ALU.mult),
                reads=["MV", "MV2"], writes=["MV3"])
            S.op("act", lambda e, nr=nr, vt=vt: e.activation(out=vt[0:nr, :], in_=vt[0:nr, :], func=AF.Identity,
                                                             bias=MV[0:nr, 3:4], scale=MV[0:nr, 2:3]),
                 reads=vk + ["MV2", "MV3"], writes=vk)
            S.op("dve", lambda e, nr=nr, vt=vt: e.tensor_tensor(out=vt[0:nr, :], in0=vt[0:nr, :], in1=LGB[0:nr, :],
                                                                op=ALU.mult),
                 reads=vk + LGK, writes=vk)
            S.op("pool", lambda e, nr=nr, vt=vt: e.tensor_tensor(out=vt[0:nr, :], in0=vt[0:nr, :], in1=LBB[0:nr, :],
                                                                 op=ALU.add),
                 reads=vk + LBK, writes=vk)
            dst = dsts[0:nr, :] if is_s else dstp[r0:r0 + nr, :]
            wkeys = [("x1s" if is_s else "x1p%d" % (r0 // 128))] if l == 0 else []
            ev = S.dma("pool", lambda e, dst=dst, nr=nr, vt=vt: e.dma_start(out=dst, in_=vt[0:nr, :]),
                       "o%d" % (ri % 3), reads=vk, writes=wkeys)
            if l == DEPTH - 1:
                out_events.append(ev)


        n_next = len(tile_rts(nxt_tile[1])) if nxt_tile is not None else 0
        for ri in range(max(len(rts4) + 1, n_next)):
            if ri < len(rts4):
                p4_stage_a(ri)
            if 1 <= ri <= len(rts4):
                p4_stage_b(ri - 1)
            if ri < n_next:
                phase0_rt(nxt_tile[0], nxt_tile[1], ri)

    def layer_end(l):
        OT = MISC
        for (SRC, ncol, dstt, skey) in ((OUTL, NL_COLS, o_l, "OUTL"), (OUTS, NS_COLS, o_s, "OUTS")):
            for h in range(NH):
                bk, bkey = bank()
                S.op("pe", lambda e, bk=bk, h=h, SRC=SRC, ncol=ncol: e.transpose(
                    out=bk[0:ncol, 0:128], in_=SRC[:, l, h, :], identity=IDN[:, :]),
                    reads=[skey, "IDN"], writes=[bkey])
                S.op("act", lambda e, bk=bk, h=h, ncol=ncol: e.copy(out=OT[0:ncol, h * 128:(h + 1) * 128],
                                                                    in_=bk[0:ncol, 0:128]),
                     reads=[bkey], writes=MISCK)
                rel(bkey)
            ev = S.dma("sp", lambda e, dstt=dstt, ncol=ncol: e.dma_start(out=dstt[l], in_=OT[0:ncol, :]),
                       "c4", reads=MISCK, writes=[])
            out_events.append(ev)

    for l in range(DEPTH):
        for tl in range(2):
            order = [(a, b) for a in range(DEPTH) for b in range(2)]
            idx_ = order.index((l, tl))
            run_tile(l, tl, idx_ == 0, order[idx_ + 1] if idx_ + 1 < len(order) else None)
        layer_end(l)
    S.final_wait("sp", out_events)
    S.final_wait("pool", out_events)

    sem_names = sorted(S.sems)
    sems = {n: es.enter_context(nc.semaphore(n)) for n in sem_names}
    block = es.enter_context(nc.Block())

    def replay(eng_handle, items):
        for it in items:
            if it[0] == "wait":
                eng_handle.wait_ge(sems[it[1]], it[2])
            else:
                _, fn, sem, inc = it
                fn(eng_handle).then_inc(sems[sem], inc)

    @block.tensor
    def _(e):
        replay(e, S.q.get("pe", []))

    @block.scalar
    def _(e):
        replay(e, S.q.get("act", []))

    @block.vector
    def _(e):
        replay(e, S.q.get("dve", []))

    @block.gpsimd
    def _(e):
        replay(e, S.q.get("pool", []))

    @block.sync
    def _(e):
        replay(e, S.q.get("sp", []))

    es.close()
    return nc


_PROG = None


def _relayout_w(w):
    K, N = w.shape
    return np.ascontiguousarray(w.reshape(K // 128, 128, N // 128, 128).transpose(2, 1, 0, 3)).reshape(
        N // 128, 128, (K // 128) * 128)


def kernel(x_prompt, x_sample, state_lru_h, state_lru_conv, state_sc_conv, p_prompt, p_sample,
           w_in, lru_conv_w, lru_conv_b, lru_wa, lru_ba, lru_wx, lru_bx, lru_lambda,
           sc_conv_w, gn_lru, gn_sc, w_out, ple_wp, ple_wg, ple_bg, ln_g, ln_b):
    in_maps, nb = _prep(x_prompt, x_sample, state_lru_h, state_lru_conv, state_sc_conv, p_prompt, p_sample,
                        w_in, lru_conv_w, lru_conv_b, lru_wa, lru_ba, lru_wx, lru_bx, lru_lambda,
                        sc_conv_w, gn_lru, gn_sc, w_out, ple_wp, ple_wg, ple_bg, ln_g, ln_b)
    global _PROG
    if _PROG is None:
        _PROG = build_program()
    res = run_bass_kernel_spmd(_PROG, in_maps, core_ids=list(range(NCORES))).results
    return _assemble(res, nb)


def _prep(x_prompt, x_sample, state_lru_h, state_lru_conv, state_sc_conv, p_prompt, p_sample,
          w_in, lru_conv_w, lru_conv_b, lru_wa, lru_ba, lru_wx, lru_bx, lru_lambda,
          sc_conv_w, gn_lru, gn_sc, w_out, ple_wp, ple_wg, ple_bg, ln_g, ln_b):
    f = lambda a: np.asarray(a, dtype=np.float32)
    x_prompt, x_sample, p_prompt, p_sample = f(x_prompt), f(x_sample), f(p_prompt), f(p_sample)
    state_lru_h, state_lru_conv, state_sc_conv = f(state_lru_h), f(state_lru_conv), f(state_sc_conv)
    nb = x_prompt.shape[0]
    win = np.stack([_relayout_w(f(w_in[l])) for l in range(DEPTH)])
    wout = np.stack([_relayout_w(f(w_out[l])) for l in range(DEPTH)])
    wgt = np.stack([_relayout_w(f(ple_wg[l])) for l in range(DEPTH)])
    wpt = np.stack([_relayout_w(f(ple_wp[l])) for l in range(DEPTH)])
    wat = np.ascontiguousarray(f(lru_wa).transpose(0, 2, 1, 3)).reshape(DEPTH, 128, 1024)
    wxt = np.ascontiguousarray(f(lru_wx).transpose(0, 2, 1, 3)).reshape(DEPTH, 128, 1024)
    cst = np.zeros((DEPTH, 128, NCST), np.float32)
    fm = lambda v: f(v).reshape(DEPTH, -1, 128).transpose(0, 2, 1)
    cst[:, :, CW:CW + 32] = f(lru_conv_w).reshape(DEPTH, 4, 8, 128).transpose(0, 3, 2, 1).reshape(DEPTH, 128, 32)
    cst[:, :, CB:CB + 8] = fm(lru_conv_b)
    cst[:, :, BA:BA + 8] = f(lru_ba).transpose(0, 2, 1)
    cst[:, :, BX:BX + 8] = f(lru_bx).transpose(0, 2, 1)
    cst[:, :, LAM:LAM + 8] = fm(lru_lambda)
    cst[:, :, SCW:SCW + 24] = f(sc_conv_w).reshape(DEPTH, 3, 8, 128).transpose(0, 3, 2, 1).reshape(DEPTH, 128, 24)
    cst[:, :, GNL:GNL + 8] = fm(gn_lru)
    cst[:, :, GNS:GNS + 8] = fm(gn_sc)
    cst[:, :, BG:BG + 16] = fm(ple_bg)
    lng, lnb = f(ln_g), f(ln_b)
    ident = np.eye(128, dtype=np.float32)
    zx = np.zeros((SEQ, D), np.float32)
    zp = np.zeros((DEPTH, SEQ, DPLE), np.float32)
    in_maps = []
    for c in range(NCORES):
        sl = slice(NSS * c, NSS * (c + 1))
        sin = np.concatenate([state_lru_conv[:, sl].reshape(DEPTH, NSS * 3, DL),
                              state_lru_h[:, sl],
                              state_sc_conv[:, sl].reshape(DEPTH, NSS * 2, DL)], axis=1)
        in_maps.append({
            "xp": x_prompt[c] if c < nb else zx,
            "xs": x_sample[sl].reshape(TS, D),
            "pp": np.ascontiguousarray(p_prompt[:, c]) if c < nb else zp,
            "ps": np.ascontiguousarray(p_sample[:, sl].reshape(DEPTH, TS, DPLE)),
            "sin": np.ascontiguousarray(sin),
            "win": win, "wout": wout, "wgt": wgt, "wpt": wpt, "wat": wat, "wxt": wxt,
            "cst": cst, "lng": lng, "lnb": lnb, "ident": ident,
        })
    return in_maps, nb


def _assemble(res, nb):
    y_prompt = np.stack([res[c]["yp"] for c in range(nb)])
    y_sample = np.concatenate([res[c]["ys"].reshape(NSS, DS, D) for c in range(NCORES)], axis=0)
    ol = [res[c]["o_l"] for c in range(NCORES)]
    os_ = [res[c]["o_s"] for c in range(NCORES)]
    lru_h_prompt = np.stack([ol[c][:, 0] for c in range(nb)], axis=1)
    lru_conv_prompt = np.stack([ol[c][:, 1:4] for c in range(nb)], axis=1)
    sc_conv_prompt = np.stack([os_[c][:, 0:2] for c in range(nb)], axis=1)
    lru_h_sample = np.concatenate([ol[c][:, 4:20] for c in range(NCORES)], axis=1)
    lru_conv_sample = np.concatenate([ol[c][:, 20:68].reshape(DEPTH, NSS, 3, DL) for c in range(NCORES)], axis=1)
    sc_conv_sample = np.concatenate([os_[c][:, 2:34].reshape(DEPTH, NSS, 2, DL) for c in range(NCORES)], axis=1)
    return (y_prompt.astype(np.float32), y_sample.astype(np.float32), lru_h_prompt, lru_conv_prompt,
            sc_conv_prompt, lru_h_sample, lru_conv_sample, sc_conv_sample)
```
